# Optimizing a Trainium2 kernel written in Bass

```python
import jax, jax.numpy as jnp
from jax import lax
import numpy as np

D_MODEL = 1024
BATCH = 4
SEQ = 8192
DEPTH = 1

PLE_DIM = 256
ROPE_THETA = 10000.0
EPS = 1e-6
NEG_INF = -1e30

MOBA_HEADS = 8
MOBA_HEAD_DIM = 64
MOBA_WIDTH = MOBA_HEADS * MOBA_HEAD_DIM
MOBA_BLOCK = 256
MOBA_TOPK = 3
MOBA_Q_CHUNK = 64

MLA_HEADS = 8
MLA_Q_RANK = 256
MLA_KV_RANK = 128
MLA_NOPE_DIM = 64
MLA_ROPE_DIM = 32
MLA_V_DIM = 64
MLA_QK_DIM = MLA_NOPE_DIM + MLA_ROPE_DIM
MLA_WIDTH = MLA_HEADS * MLA_V_DIM
MLA_Q_CHUNK = 128

D_FF = -(-8 * D_MODEL // (3 * 256)) * 256

IN_SIZES = (MOBA_WIDTH, MOBA_WIDTH, MOBA_WIDTH,
            MLA_Q_RANK, MLA_KV_RANK, MLA_ROPE_DIM,
            D_MODEL, D_MODEL)
IN_COLS = sum(IN_SIZES)

kernel_name = "hybrid_moba_mla_gated_block"


def _split_cols(t, sizes):
    offs = np.cumsum(sizes)[:-1].tolist()
    return jnp.split(t, offs, axis=-1)


def rmsnorm(x, g):
    xf = x.astype(jnp.float32)
    xf = xf * lax.rsqrt(jnp.mean(xf * xf, axis=-1, keepdims=True) + EPS)
    return (xf * g.astype(jnp.float32)).astype(x.dtype)


def rope(x, pos):
    d = x.shape[-1]
    half = d // 2
    inv_freq = 1.0 / (ROPE_THETA ** (jnp.arange(half, dtype=jnp.float32) * (2.0 / d)))
    ang = pos.astype(jnp.float32)[:, None] * inv_freq[None, :]
    cos = jnp.cos(ang).astype(x.dtype)
    sin = jnp.sin(ang).astype(x.dtype)
    x1, x2 = x[..., :half], x[..., half:]
    return jnp.concatenate([x1 * cos - x2 * sin, x2 * cos + x1 * sin], axis=-1)


def split_heads(t, n_heads):
    b, s, w = t.shape
    return t.reshape(b, s, n_heads, w // n_heads).transpose(0, 2, 1, 3)


def merge_heads(t):
    b, h, s, d = t.shape
    return t.transpose(0, 2, 1, 3).reshape(b, s, h * d)


def moba_attention(q, k, v):
    B, H, S, Dh = q.shape
    nb = -(-S // MOBA_BLOCK)
    pad = nb * MOBA_BLOCK - S
    kp = jnp.pad(k, ((0, 0), (0, 0), (0, pad), (0, 0)))
    vp = jnp.pad(v, ((0, 0), (0, 0), (0, pad), (0, 0)))
    kb = kp.reshape(B, H, nb, MOBA_BLOCK, Dh)
    vb = vp.reshape(B, H, nb, MOBA_BLOCK, Dh)
    k_mean = jnp.mean(kb.astype(jnp.float32), axis=3)
    k_sel_n = min(MOBA_TOPK, nb)
    scale = Dh ** -0.5
    n_chunks = S // MOBA_Q_CHUNK
    gather_blocks = jax.vmap(jax.vmap(lambda blocks, idx: blocks[idx]))

    def chunk(c):
        q0 = c * MOBA_Q_CHUNK
        qc = lax.dynamic_slice_in_dim(q, q0, MOBA_Q_CHUNK, axis=2)
        qpos = q0 + jnp.arange(MOBA_Q_CHUNK)
        cur = q0 // MOBA_BLOCK
        gate = jnp.einsum('bhqd,bhnd->bhqn', qc.astype(jnp.float32), k_mean)
        past = jnp.arange(nb) < cur
        gate = jnp.where(past[None, None, None, :], gate, NEG_INF)
        gval, idx = lax.top_k(gate, k_sel_n)
        valid = gval > (NEG_INF * 0.5)
        k_sel = gather_blocks(kb, idx)
        v_sel = gather_blocks(vb, idx)
        s_sel = jnp.einsum('bhqd,bhqkjd->bhqkj', qc, k_sel).astype(jnp.float32) * scale
        s_sel = jnp.where(valid[..., None], s_sel, NEG_INF)
        s_sel = s_sel.reshape(B, H, MOBA_Q_CHUNK, k_sel_n * MOBA_BLOCK)
        k_own = lax.dynamic_index_in_dim(kb, cur, axis=2, keepdims=False)
        v_own = lax.dynamic_index_in_dim(vb, cur, axis=2, keepdims=False)
        kpos = cur * MOBA_BLOCK + jnp.arange(MOBA_BLOCK)
        s_own = jnp.einsum('bhqd,bhjd->bhqj', qc, k_own).astype(jnp.float32) * scale
        s_own = jnp.where((kpos[None, :] <= qpos[:, None])[None, None], s_own, NEG_INF)
        probs = jax.nn.softmax(jnp.concatenate([s_sel, s_own], axis=-1), axis=-1).astype(v.dtype)
        p_sel = probs[..., :k_sel_n * MOBA_BLOCK].reshape(B, H, MOBA_Q_CHUNK, k_sel_n, MOBA_BLOCK)
        p_own = probs[..., k_sel_n * MOBA_BLOCK:]
        return (jnp.einsum('bhqkj,bhqkjd->bhqd', p_sel, v_sel)
                + jnp.einsum('bhqj,bhjd->bhqd', p_own, v_own))

    out = lax.map(chunk, jnp.arange(n_chunks))
    return jnp.moveaxis(out, 0, 2).reshape(B, H, S, Dh)


def mla_attention(q_nope, q_rope, k_nope, k_rope, v):
    B, H, S, _ = q_nope.shape
    scale = MLA_QK_DIM ** -0.5
    kpos = jnp.arange(S)
    n_chunks = S // MLA_Q_CHUNK

    def chunk(c):
        q0 = c * MLA_Q_CHUNK
        qn = lax.dynamic_slice_in_dim(q_nope, q0, MLA_Q_CHUNK, axis=2)
        qr = lax.dynamic_slice_in_dim(q_rope, q0, MLA_Q_CHUNK, axis=2)
        s = (jnp.einsum('bhqd,bhkd->bhqk', qn, k_nope)
             + jnp.einsum('bhqd,bkd->bhqk', qr, k_rope)).astype(jnp.float32) * scale
        qpos = q0 + jnp.arange(MLA_Q_CHUNK)
        s = jnp.where((kpos[None, :] <= qpos[:, None])[None, None], s, NEG_INF)
        probs = jax.nn.softmax(s, axis=-1).astype(v.dtype)
        return jnp.einsum('bhqk,bhkd->bhqd', probs, v)

    out = lax.map(chunk, jnp.arange(n_chunks))
    return jnp.moveaxis(out, 0, 2).reshape(B, H, S, MLA_V_DIM)


def setup_inputs(seed: int = 0) -> dict:
    key = jax.random.key(seed)
    ks = jax.random.split(key, 20)

    def w(k, shape, fan_in):
        return jax.random.normal(k, shape, jnp.float32) * (fan_in ** -0.5)

    def gain(k, shape):
        return 1.0 + 0.05 * jax.random.normal(k, shape, jnp.float32)

    return {
        "x": jax.random.normal(ks[0], (BATCH, SEQ, D_MODEL), jnp.float32),
        "p": jax.random.normal(ks[1], (DEPTH, BATCH, SEQ, PLE_DIM), jnp.float32),
        "attn_norm": gain(ks[2], (DEPTH, D_MODEL)),
        "w_in": w(ks[3], (DEPTH, D_MODEL, IN_COLS), D_MODEL),
        "mla_q_norm": gain(ks[4], (DEPTH, MLA_Q_RANK)),
        "w_q_b": w(ks[5], (DEPTH, MLA_Q_RANK, MLA_HEADS * MLA_QK_DIM), MLA_Q_RANK),
        "mla_kv_norm": gain(ks[6], (DEPTH, MLA_KV_RANK)),
        "w_kv_b": w(ks[7], (DEPTH, MLA_KV_RANK, MLA_HEADS * (MLA_NOPE_DIM + MLA_V_DIM)), MLA_KV_RANK),
        "w_moba_branch": w(ks[8], (DEPTH, MOBA_WIDTH, D_MODEL), MOBA_WIDTH),
        "w_mla_branch": w(ks[9], (DEPTH, MLA_WIDTH, D_MODEL), MLA_WIDTH),
        "w_o": w(ks[10], (DEPTH, D_MODEL, D_MODEL), D_MODEL),
        "ffn_norm": gain(ks[11], (DEPTH, D_MODEL)),
        "w_gate_up": w(ks[12], (DEPTH, D_MODEL, 2 * D_FF), D_MODEL),
        "w_down": w(ks[13], (DEPTH, D_FF, D_MODEL), D_FF),
        "ple_norm": gain(ks[14], (DEPTH, D_MODEL)),
        "w_ple_gate": w(ks[15], (DEPTH, D_MODEL, D_MODEL), D_MODEL),
        "w_ple_proj": w(ks[16], (DEPTH, PLE_DIM, D_MODEL), PLE_DIM),
        "final_norm": gain(ks[17], (D_MODEL,)),
    }


def reference(x, p, attn_norm, w_in, mla_q_norm, w_q_b, mla_kv_norm, w_kv_b,
              w_moba_branch, w_mla_branch, w_o, ffn_norm, w_gate_up, w_down,
              ple_norm, w_ple_gate, w_ple_proj, final_norm):
    B, S, _ = x.shape
    pos = jnp.arange(S)
    for i in range(DEPTH):
        h = rmsnorm(x, attn_norm[i])
        proj = h @ w_in[i]
        qa, ka, va, cq, ckv, kr, ga, gb = _split_cols(proj, IN_SIZES)

        qa = rope(split_heads(qa, MOBA_HEADS), pos)
        ka = rope(split_heads(ka, MOBA_HEADS), pos)
        va = split_heads(va, MOBA_HEADS)
        ya = merge_heads(moba_attention(qa, ka, va)) @ w_moba_branch[i]

        qm = split_heads(rmsnorm(cq, mla_q_norm[i]) @ w_q_b[i], MLA_HEADS)
        q_nope, q_rope = qm[..., :MLA_NOPE_DIM], rope(qm[..., MLA_NOPE_DIM:], pos)
        kvm = split_heads(rmsnorm(ckv, mla_kv_norm[i]) @ w_kv_b[i], MLA_HEADS)
        k_nope, vm = kvm[..., :MLA_NOPE_DIM], kvm[..., MLA_NOPE_DIM:]
        k_rope = rope(kr, pos)
        yb = merge_heads(mla_attention(q_nope, q_rope, k_nope, k_rope, vm)) @ w_mla_branch[i]

        mix = jax.nn.sigmoid(ga) * ya + jax.nn.sigmoid(gb) * yb
        x = x + mix @ w_o[i]

        h = rmsnorm(x, ffn_norm[i])
        g, u = jnp.split(h @ w_gate_up[i], 2, axis=-1)
        x = x + (jax.nn.silu(g) * u) @ w_down[i]

        h = rmsnorm(x, ple_norm[i])
        x = x + jax.nn.sigmoid(h @ w_ple_gate[i]) * (p[i] @ w_ple_proj[i])
    return rmsnorm(x, final_norm)
```

```python
import numpy as np
import ml_dtypes
from contextlib import ExitStack
import concourse.bass as bass
import concourse.mybir as mybir
from concourse.bass_utils import run_bass_kernel_spmd

F32 = mybir.dt.float32
BF16 = mybir.dt.bfloat16
AF = mybir.ActivationFunctionType
ALU = mybir.AluOpType
AX = mybir.AxisListType

D = 1024
KC = 8
DFF = 2816
NEG = -30000.0
EPS = 1e-6
OFFS = ([0, 3, 4, 7], [1, 2, 5, 6])
SC_MOBA = 64 ** -0.5
SC_MLA = 96 ** -0.5


class T:
    __slots__ = ("name", "w", "r", "multi", "sem", "semcnt")

    def __init__(self, name, multi=False):
        self.name = name
        self.w = []
        self.r = []
        self.multi = multi
        self.sem = None
        self.semcnt = 0


class Op:
    __slots__ = ("eng", "fn", "deps", "sig", "dma", "owner", "sigval", "idx")

    def __init__(self, eng, fn, dma, owner):
        self.eng = eng
        self.fn = fn
        self.deps = {}
        self.sig = False
        self.dma = dma
        self.owner = owner
        self.sigval = None


class Sched:
    ENGS = ("pe", "act", "dve", "pool", "sp")

    def __init__(self):
        self.ops = []

    def _dep(self, op, p, raw):
        if p is op:
            return
        if (not p.dma) and (not op.dma) and p.eng == op.eng:
            if p.eng == "pe" or (not raw and p.eng != "pool"):
                return
        op.deps[id(p)] = p

    def add(self, eng, fn, reads=(), writes=(), dma=False, owner=None):
        op = Op(eng, fn, dma, owner)
        for t in reads:
            for w in t.w:
                self._dep(op, w, True)
        for t in writes:
            if not t.multi:
                for w in t.w:
                    if dma and w.dma and w.owner is owner:
                        continue
                    self._dep(op, w, False)
            for r in t.r:
                self._dep(op, r, False)
        for t in reads:
            if not dma:
                t.r = [x for x in t.r if x.dma or x.eng != eng]
            t.r.append(op)
        for t in writes:
            if t.multi:
                t.w.append(op)
            else:
                t.w = [op]
                t.r = []
        self.ops.append(op)
        return op

    def emit(self, nc, stack, handles_fn):
        for op in self.ops:
            for p in op.deps.values():
                p.sig = True
        sems = {}
        for e in self.ENGS:
            sems[e] = stack.enter_context(nc.semaphore("s_" + e))
        cnt = {e: 0 for e in self.ENGS}
        for op in self.ops:
            if op.dma:
                ow = op.owner
                if ow.sem is None:
                    ow.sem = stack.enter_context(nc.semaphore("d_" + ow.name))
                ow.semcnt += 16
                op.sigval = (ow.sem, ow.semcnt)
            elif op.sig:
                cnt[op.eng] += 1
                op.sigval = (sems[op.eng], cnt[op.eng])
        per = {e: [o for o in self.ops if o.eng == e] for e in self.ENGS}
        fw = {}
        for op in self.ops:
            if op.dma:
                fw[id(op.owner)] = (op.owner.sem, op.owner.semcnt)
        self.final_waits = list(fw.values())

        def run(e, h):
            waited = {}
            for op in per[e]:
                need = {}
                for p in op.deps.values():
                    s, v = p.sigval
                    k = id(s)
                    if k not in need or need[k][1] < v:
                        need[k] = (s, v)
                for k, (s, v) in need.items():
                    if waited.get(k, 0) >= v:
                        continue
                    h.wait_ge(s, v)
                    waited[k] = v
                ins = op.fn(h)
                if op.dma:
                    ins.then_inc(op.sigval[0], 16)
                elif op.sig:
                    ins.then_inc(op.sigval[0], 1)
            if e == "sp":
                for (s, v) in self.final_waits:
                    h.wait_ge(s, v)

        handles_fn(run)


def build(S, dbg=False, phases=("A", "M")):
    NT = S // 512
    NG = S // 1024
    SO = S // 2
    NB = 32
    nc = bass.Bass("TRN2", target_bir_lowering=False)
    sch = Sched()
    marks = []

    def mark(name):
        marks.append((name, {e: sum(1 for o in sch.ops if o.eng == e) for e in Sched.ENGS}))
    st = ExitStack()

    def dram(name, shape, dt, kind="ExternalInput"):
        return nc.dram_tensor(name, list(shape), dt, kind=kind).ap()

    xT_all = dram("xT_all", [D, S], F32)
    xT_own = dram("xT_own", [D, SO], F32)
    pT_own = dram("pT_own", [256, SO], F32)
    tabK = dram("tabK", [128, 4, S], F32)
    tabQ = dram("tabQ", [128, 4, SO], F32)
    onehot = dram("onehot", [NB, S], BF16)
    diag_d = dram("diag", [128, 8, 512], BF16)
    ident_d = dram("ident", [128, 128], BF16)
    gains_d = dram("gains", [128, 35], F32)
    w_kside = dram("w_kside", [D, 1728], F32)
    w_qside = dram("w_qside", [D, 1280], F32)
    w_gates = dram("w_gates", [D, 2048], F32)
    w_kvb_k = dram("w_kvb_k", [128, 512], F32)
    w_kvb_v = dram("w_kvb_v", [128, 512], F32)
    w_qb = dram("w_qb", [256, 1536], F32)
    w_ba = dram("w_ba", [512, D], F32)
    w_bb = dram("w_bb", [512, D], F32)
    w_o = dram("w_o", [D, D], F32)
    w_gu = dram("w_gu", [D, 2 * DFF], F32)
    w_dn = dram("w_dn", [DFF, D], F32)
    w_pg = dram("w_pg", [D, D], F32)
    w_pp = dram("w_pp", [256, D], F32)
    outT = dram("outT", [D, SO], F32, kind="ExternalOutput")

    SK = "ExternalOutput" if dbg else "Internal"
    KaT = dram("KaT", [8, 64, S], BF16, kind=SK)
    KnT = dram("KnT", [8, 64, S], BF16, kind=SK)
    KrT = dram("KrT", [32, S], BF16, kind=SK)
    Vas = dram("Vas", [8, S // 1024, 128, 8, 65], BF16, kind=SK)
    Vms = dram("Vms", [8, S // 1024, 128, 8, 65], BF16, kind=SK)
    ksum_d = dram("ksum_d", [128, 4, NB], F32, kind=SK)
    wb = {}
    WLIST = (("w_qside", w_qside), ("w_qb", w_qb), ("w_gates", w_gates), ("w_ba", w_ba), ("w_bb", w_bb),
             ("w_o", w_o), ("w_gu", w_gu), ("w_dn", w_dn), ("w_pg", w_pg), ("w_pp", w_pp))
    for nm, src in WLIST:
        wb[nm] = dram(nm + "_bf", list(src.shape), BF16, kind="Internal")
    dbg_out = {}

    def sb(name, shape, dt):
        return st.enter_context(nc.sbuf_tensor("sb_" + name, list(shape), dt))

    def ps(name, shape, dt):
        return st.enter_context(nc.psum_tensor("ps_" + name, list(shape), dt))

    ident = sb("ident", [128, 128], BF16)
    ones = sb("ones", [128, 128], BF16)
    zeros = sb("zeros", [128, 384], BF16)
    gains = sb("gains", [128, 35], F32)
    diag = sb("diag", [128, 8, 512], BF16)
    wbuf = sb("wbuf", [128, 3 * 4096], BF16)
    WkB = sb("WkB", [128, KC * 192 + 1024], BF16)
    xs = [sb("xs%d" % i, [128, KC, 512], F32) for i in range(2)]
    hbuf = sb("hbuf", [128, KC, 512], BF16)
    hbuf2 = sb("hbuf2", [128, KC, 512], BF16)
    rstd = sb("rstd", [128, 512], F32)
    rtmp = sb("rtmp", [128, 512], F32)
    t1b = [sb("t1b%d" % i, [128, 512], F32) for i in range(2)]
    t2b = [sb("t2b%d" % i, [128, 512], F32) for i in range(2)]
    sqc = sb("sqc", [128, 2, 512], BF16)
    tab = sb("tab", [128, 4, 512], F32)
    shared = sb("shared", [128, 22 * 512], BF16)
    kast = shared[:, 0:2048].rearrange("p (a n) -> p a n", a=4)
    knst = shared[:, 2048:4096].rearrange("p (a n) -> p a n", a=4)
    vast = shared[:, 4096:6176].rearrange("p (h c d) -> p h c d", h=8, c=4)
    vmst = shared[:, 6176:8256].rearrange("p (h c d) -> p h c d", h=8, c=4)
    krst = shared[0:32, 8256:8768]
    sm = sb("sm", [128, 16], F32)
    cnT = sb("cnT", [128, 2, 512], BF16)
    ksum = sb("ksum", [128, 4, NB], F32)
    kmean = sb("kmean", [64, 8, NB], BF16)
    kmean_f = sb("kmean_f", [64, 8, NB], F32)

    T_const = T("const")
    T_ones = T("ones")
    T_wslot = [T("wslot%d" % i) for i in range(3)]
    T_WkB = T("WkB")
    T_wkA = T("wkA")
    T_xs = [T("xs0"), T("xs1")]
    T_h = T("hbuf")
    T_h2 = T("hbuf2")
    T_rstd = T("rstd")
    T_rtmp = T("rtmp")
    T_t1 = [T("t1b0"), T("t1b1")]
    T_t2 = [T("t2b0"), T("t2b1")]
    T_sqc = T("sqc")
    T_tab = T("tab")
    T_tabq = T("tabq")
    T_kast, T_knst, T_vast, T_vmst, T_krst = T("kast"), T("knst"), T("vast"), T("vmst"), T("krst")
    T_sm = T("sm")
    T_cnT = T("cnT")
    T_ksum, T_kmean = T("ksum"), T("kmean")
    T_scr = T("scratch", multi=True)
    T_wscr = T("wscratch", multi=True)
    T_wm = {}
    T_ksd = T("ksum_d")

    pA = ps("pA", [128, 1536], F32)
    pB = ps("pB", [128, 512], F32)
    pC = ps("pC", [128, 1024], BF16)
    pD = ps("pD", [128, 512], F32)
    pE = ps("pE", [128, 512], F32)
    pF = ps("pF", [128, 512], F32)
    T_pA = [T("pA0"), T("pA1"), T("pA2")]
    T_pB, T_pC, T_pD, T_pE, T_pF = T("pB"), T("pC"), T("pD"), T("pE"), T("pF")

    def dma(eng, out, in_, reads, writes, owner):
        sch.add(eng, lambda e: e.dma_start(out=out, in_=in_), reads, writes, dma=True, owner=owner)

    def mm(out, lhsT, rhs, start, stop, reads, writes):
        sch.add("pe", lambda e: e.matmul(out, lhsT, rhs, start=start, stop=stop), reads, writes)

    def tp(out, in_, reads, writes):
        sch.add("pe", lambda e: e.transpose(out, in_, ident[0:in_.shape[0], 0:in_.shape[0]]),
                list(reads) + [T_const], writes)

    def act(out, in_, func, reads, writes, scale=1.0, bias=0.0, accum=None):
        if accum is None:
            sch.add("act", lambda e: e.activation(out, in_, func, bias=bias, scale=scale), reads, writes)
        else:
            sch.add("act", lambda e: e.activation(out, in_, func, bias=bias, scale=scale, accum_out=accum),
                    reads, writes)

    def tt(eng, out, in0, in1, op, reads, writes):
        sch.add(eng, lambda e: e.tensor_tensor(out, in0, in1, op), reads, writes)

    def ts(eng, out, in0, s1, s2, op0, op1, reads, writes):
        if op1 is None:
            sch.add(eng, lambda e: e.tensor_scalar(out, in0, s1, None, op0), reads, writes)
        else:
            sch.add(eng, lambda e: e.tensor_scalar(out, in0, s1, s2, op0, op1), reads, writes)

    def stt(out, in0, scalar, in1, op0, op1, reads, writes):
        sch.add("dve", lambda e: e.scalar_tensor_tensor(out, in0, scalar, in1, op0, op1), reads, writes)

    def cp(eng, out, in_, reads, writes):
        if eng == "act":
            sch.add("act", lambda e: e.copy(out, in_), reads, writes)
        else:
            sch.add(eng, lambda e: e.tensor_copy(out, in_), reads, writes)

    def recip(out, in_, reads, writes):
        sch.add("dve", lambda e: e.reciprocal(out, in_), reads, writes)

    def memset(eng, ap, val, writes):
        sch.add(eng, lambda e: e.memset(ap, val), (), writes)

    def rsqrt_chain(out, in_, n_inv, reads_in, T_out, T_tmp, tmp):
        ts("dve", tmp, in_, n_inv, EPS, ALU.mult, ALU.add, reads_in, [T_tmp])
        act(tmp, tmp, AF.Ln, [T_tmp], [T_tmp])
        act(out, tmp, AF.Exp, [T_tmp], [T_out], scale=-0.5)

    dma("sp", ident[:], ident_d, (), [T_const], T_const)
    dma("sp", gains[:], gains_d, (), [T_const], T_const)
    dma("sp", diag[:], diag_d, (), [T_const], T_const)
    memset("dve", ones[:], 1.0, [T_ones])
    memset("dve", zeros[:], 0.0, [T_ones])
    memset("dve", ksum[:], 0.0, [T_ksum])
    memset("pool", vast, 1.0, [T_vast])
    memset("pool", vmst, 1.0, [T_vmst])
    Wk = wbuf[:, 0:KC * 1536].rearrange("p (k n) -> p k n", k=KC)
    WkC = WkB[:, 0:KC * 192].rearrange("p (k n) -> p k n", k=KC)
    Wkk = WkB[:, KC * 192:KC * 192 + 512]
    Wkv = WkB[:, KC * 192 + 512:KC * 192 + 1024]
    dma("sp", xs[0][:], xT_all.rearrange("(k p) n -> p k n", p=128)[:, :, 0:512], (), [T_xs[0]], T_xs[0])
    dma("pool", tab[:], tabK[:, :, 0:512], (), [T_tab], T_tab)
    wsrc = w_kside.rearrange("(k p) n -> p k n", p=128)
    for k in range(KC):
        dma("pool", WkC[:, k, :], wsrc[:, k, 1536:1728], (), [T_WkB], T_WkB)
    dma("pool", Wkk, w_kvb_k, (), [T_WkB], T_WkB)
    dma("pool", Wkv, w_kvb_v, (), [T_WkB], T_WkB)
    for k in range(KC):
        dma("pool", Wk[:, k, 0:512], wsrc[:, k, 0:512], (), T_wslot, T_wkA)
    stg = xs[1][:].rearrange("p k n -> p (k n)")
    T_stg = T_xs[1]
    for blk in (1, 2):
        for hf in range(2):
            dma("sp", stg[:, hf * 2048:(hf + 1) * 2048].rearrange("p (k n) -> p k n", k=4),
                wsrc[:, hf * 4:hf * 4 + 4, blk * 512:(blk + 1) * 512], (), [T_stg], T_stg)
        cp("dve", Wk[:, :, blk * 512:(blk + 1) * 512], stg.rearrange("p (k n) -> p k n", k=KC), [T_stg], T_wslot)
    T_wall = T_wslot + [T_WkB]
    if "M" in phases:
        first = True
        for nm, src in WLIST:
            rows = src.shape[0]
            T_wm[nm] = T("wscr_" + nm, multi=True)
            for r0 in range(0, rows, 128):
                r1 = min(rows, r0 + 128)
                dma("pool", wb[nm][r0:r1, :], src[r0:r1, :], (T_wslot if first else ()), [T_wm[nm]], T_wscr)
                first = False

    def make_hT(xt, T_x, gcol0, hb=None, T_hb=None):
        if hb is None:
            hb, T_hb = hbuf, T_h
        for hf in range(2):
            act(hb[:, hf * 4:hf * 4 + 4, :].rearrange("p k n -> p (k n)"),
                xt[:, hf * 4:hf * 4 + 4, :].rearrange("p k n -> p (k n)"), AF.Square, [T_x], [T_hb])
            for k in range(hf * 4, hf * 4 + 4):
                mm(pB[:], ones[:], hb[:, k, :], k == 0, k == KC - 1, [T_hb, T_ones], [T_pB])
        rsqrt_chain(rstd[:], pB[:], 1.0 / D, [T_pB], T_rstd, T_rtmp, rtmp[:])
        for k in range(KC):
            stt(hb[:, k, :], xt[:, k, :], gains[:, gcol0 + k:gcol0 + k + 1], rstd[:], ALU.mult, ALU.mult,
                [T_x, T_rstd, T_const], [T_hb])

    rr = {"i": 0, "b": 0}

    def rope_fm(p1, p2, cosT, sinT, out, T_p1, T_p2, T_o):
        i = rr["i"] % 2
        rr["i"] += 1
        np_ = p1.shape[0]
        b0 = p1.base_partition() if hasattr(p1, "base_partition") else 0
        a1 = t1b[i][b0:b0 + np_, :]
        a2 = t2b[i][b0:b0 + np_, :]
        tt("dve", a1, p1, cosT, ALU.mult, [T_p1, T_tab], [T_t1[i]])
        tt("dve", a2, p2, sinT, ALU.mult, [T_p2, T_tab], [T_t2[i]])
        tt("pool", out, a1, a2, ALU.add, [T_t1[i], T_t2[i]], [T_o])

    gbanks = None

    def gbank():
        b_ = gbanks[rr["b"] % len(gbanks)]
        rr["b"] += 1
        return b_

    mark('start')
    gbanks = [(pA[:, 0:512], T_pA[0]), (pA[:, 512:1024], T_pA[1]), (pA[:, 1024:1536], T_pA[2]),
              (pF[:], T_pF), (pD[:], T_pD), (pE[:], T_pE)]
    if "A" in phases:
        def a_load_x(t_):
            dma("sp", xs[t_ % 2][:], xT_all.rearrange("(k p) n -> p k n", p=128)[:, :, t_ * 512:t_ * 512 + 512],
                (), [T_xs[t_ % 2]], T_xs[t_ % 2])

        def a_load_tab(t_):
            dma("pool", tab[:], tabK[:, :, t_ * 512:t_ * 512 + 512], (), [T_tab], T_tab)

        hbs = [(hbuf, T_h), (hbuf2, T_h2)]
        if NT > 1:
            a_load_x(1)
        make_hT(xs[0], T_xs[0], 0, hbuf, T_h)
        for t in range(NT):
            xt, T_x = xs[t % 2], T_xs[t % 2]
            hb, T_hb = hbs[t % 2]
            t0 = t * 512
            pc, T_pc = gbank()
            for k in range(KC):
                mm(pc, WkC[:, k, 0:128], hb[:, k, :], k == 0, k == KC - 1, [T_hb] + T_wall, [T_pc])
            act(sqc[:, 0, :], pc, AF.Square, [T_pc], [T_sqc])
            mm(pB[:], ones[:], sqc[:, 0, :], True, True, [T_sqc, T_ones], [T_pB])
            rsqrt_chain(rstd[:], pB[:], 1.0 / 128, [T_pB], T_rstd, T_rtmp, rtmp[:])
            stt(cnT[:, 0, :], pc, gains[:, 32:33], rstd[:], ALU.mult, ALU.mult, [T_pc, T_rstd, T_const], [T_cnT])
            for pr in range(4):
                p1, T_p1 = gbank()
                for k in range(KC):
                    mm(p1, Wk[:, k, pr * 128:(pr + 1) * 128], hb[:, k, :], k == 0, k == KC - 1,
                       [T_hb] + T_wall, [T_p1])
                p2, T_p2 = gbank()
                for k in range(KC):
                    mm(p2, Wk[:, k, 512 + pr * 128:512 + (pr + 1) * 128], hb[:, k, :], k == 0, k == KC - 1,
                       [T_hb] + T_wall, [T_p2])
                rope_fm(p1, p2, tab[:, 0, :], tab[:, 1, :], kast[:, pr, :], T_p1, T_p2, T_kast)
            if t + 1 < NT:
                make_hT(xs[(t + 1) % 2], T_xs[(t + 1) % 2], 0, hbs[(t + 1) % 2][0], hbs[(t + 1) % 2][1])
            for s_ in range(4):
                tk = slice(s_ * 128, (s_ + 1) * 128)
                pv, T_pv = gbank()
                for k in range(KC):
                    mm(pv, hb[:, k, tk], Wk[:, k, 1024:1536], k == 0, k == KC - 1, [T_hb] + T_wall, [T_pv])
                cp("act", vast[:, :, s_, 0:64], pv.rearrange("p (h d) -> p h d", h=8), [T_pv], [T_vast])
            for pr in range(4):
                pk, T_pk = gbank()
                mm(pk, Wkk[:, pr * 128:(pr + 1) * 128], cnT[:, 0, :], True, True, [T_cnT] + T_wall, [T_pk])
                cp("act", knst[:, pr, :], pk, [T_pk], [T_knst])
            for s_ in range(4):
                tk = slice(s_ * 128, (s_ + 1) * 128)
                pv, T_pv = gbank()
                mm(pv, cnT[:, 0, tk], Wkv, True, True, [T_cnT] + T_wall, [T_pv])
                cp("dve", vmst[:, :, s_, 0:64], pv.rearrange("p (h d) -> p h d", h=8), [T_pv], [T_vmst])
            p1, T_p1 = gbank()
            for k in range(KC):
                mm(p1[0:32, :], WkC[:, k, 128:160], hb[:, k, :], k == 0, k == KC - 1, [T_hb] + T_wall, [T_p1])
            p2, T_p2 = gbank()
            for k in range(KC):
                mm(p2[0:32, :], WkC[:, k, 160:192], hb[:, k, :], k == 0, k == KC - 1, [T_hb] + T_wall, [T_p2])
            rope_fm(p1[0:32, :], p2[0:32, :], tab[0:32, 2, :], tab[0:32, 3, :], krst, T_p1, T_p2, T_krst)
            if t + 1 < NT:
                a_load_tab(t + 1)
            if t + 2 < NT:
                a_load_x(t + 2)
            sch.add("dve", lambda e, t=t: e.tensor_reduce(
                ksum[:, :, 2 * t:2 * t + 2], kast.rearrange("p a (b j) -> p a b j", b=2), AX.X, ALU.add),
                [T_kast], [T_ksum])
            dma("sp", KaT.rearrange("(a two) d n -> (two d) a n", two=2)[:, :, t0:t0 + 512], kast,
                [T_kast], [T_scr], T_kast)
            dma("sp", KnT.rearrange("(a two) d n -> (two d) a n", two=2)[:, :, t0:t0 + 512], knst,
                [T_knst], [T_scr], T_knst)
            dma("sp", KrT[:, t0:t0 + 512], krst, [T_krst], [T_scr], T_krst)
            g, hs = t // 2, (t % 2) * 4
            dma("sp", Vas[:, g, :, hs:hs + 4, :].rearrange("h p c d -> p h c d"), vast, [T_vast], [T_scr],
                T_vast)
            dma("sp", Vms[:, g, :, hs:hs + 4, :].rearrange("h p c d -> p h c d"), vmst, [T_vmst], [T_scr],
                T_vmst)
        dma("sp", ksum_d, ksum[:], [T_ksum], [T_ksd], T_ksum)
        dma("sp", kmean_f[:].rearrange("d (a two) n -> d a two n", two=2),
            ksum_d.rearrange("(two d) a n -> d a two n", two=2), [T_ksd], [T_kmean], T_kmean)
        ts("dve", kmean[:], kmean_f[:], 1.0 / 256, None, ALU.mult, None, [T_kmean], [T_kmean])

    if "M" in phases:
        Qa = sb("Qa", [96, 8, 512], BF16)
        Qm = sb("Qm", [96, 8, 512], BF16)
        sig = sb("sig", [128, 16, 512], BF16)
        ksl = [sb("ksl%d" % i, [96, 1024], BF16) for i in range(3)]
        vsl = [sb("vsl%d" % i, [128, 8, 65], BF16) for i in range(3)]
        PT = [sb("PT%d" % i, [128, 512], BF16) for i in range(3)]
        attn_tok = [sb("attn_tok%d" % i, [128, 4, 512], BF16) for i in range(2)]
        attnT = sb("attnT", [128, 2, 4, 512], BF16)
        gate_sb = sb("gate_sb", [128, 8, 32], F32)
        top8 = sb("top8", [128, 8, 8], F32)
        self_ = sb("self", [128, 8, 32], F32)
        selpad = sb("selpad", [128, 8, 128], BF16)
        actT = shared[:, :].rearrange("p (j n) -> p j n", j=22)
        tmp2 = sb("tmp2", [128, 512], F32)
        pt_bf = sb("pt_bf", [128, 2, 512], BF16)
        Wpp = sb("Wpp", [128, 2, 1024], BF16)
        T_Wpp = T("Wpp")
        dma("pool", Wpp[:], w_pp.rearrange("(k p) n -> p k n", p=128), (), [T_Wpp], T_Wpp)
        tmpf = rtmp
        T_tmpf = T_rtmp
        T_Qa, T_Qm, T_sig = T("Qa"), T("Qm"), T("sig")
        T_ks = [T("ks%d" % i) for i in range(3)]
        T_vs = [T("vs%d" % i) for i in range(3)]
        T_PT = [T("PT%d" % i) for i in range(3)]
        T_atok, T_attnT = [T("attn_tok0"), T("attn_tok1")], T("attnT")
        T_sm4 = [T("sm4a"), T("sm4b")]
        T_gsb, T_top8, T_self, T_selpad = T("gate_sb"), T("top8"), T("self"), T("selpad")
        T_actT, T_tmp2, T_ptbf = T("actT"), T("tmp2"), T("pt_bf")
        T_out = T("out", multi=True)
        T_xst = [T("xst0"), T("xst1")]

        memset("pool", gate_sb[:], -1e30, [T_gsb])
        memset("pool", selpad[:], 0.0, [T_selpad])
        memset("pool", Qa[:], 0.0, [T_Qa])

        def v_k(name, k0, k1, c0, c1):
            return wb[name].rearrange("(k p) n -> p k n", p=128)[:, k0:k1, c0:c1], (k1 - k0, c1 - c0), T_wm[name]
        plan = []
        plan.append(v_k("w_qside", 0, 8, 1024, 1280))
        plan.append(v_k("w_qside", 0, 8, 0, 512))
        plan.append(v_k("w_qside", 0, 8, 512, 1024))
        plan.append(v_k("w_qb", 0, 2, 0, 1536))
        for j in range(4):
            plan.append(v_k("w_gates", 0, 8, j * 512, (j + 1) * 512))
        for j in range(2):
            plan.append(v_k("w_ba", 0, 4, j * 512, (j + 1) * 512))
            plan.append(v_k("w_bb", 0, 4, j * 512, (j + 1) * 512))
        for j in range(2):
            plan.append(v_k("w_o", 0, 8, j * 512, (j + 1) * 512))
        for j in range(11):
            plan.append((wb["w_gu"].rearrange("(k p) (t n) -> p k t n", p=128, t=2)[:, :, :, j * 256:(j + 1) * 256],
                         (8, 2, 256), T_wm["w_gu"]))
        for cpn in range(2):
            for kg in range(3):
                plan.append(v_k("w_dn", kg * 8, min(22, kg * 8 + 8), cpn * 512, (cpn + 1) * 512))
        for j in range(2):
            plan.append(v_k("w_pg", 0, 8, j * 512, (j + 1) * 512))
        NP = len(plan)
        wstate = {"i": 0, "c": 0}
        WTOT = NP * NG

        def wview(slot, shp):
            n = 1
            for d_ in shp:
                n *= d_
            flat = wbuf[:, slot * 4096:slot * 4096 + n]
            if len(shp) == 2:
                return flat.rearrange("p (k n) -> p k n", k=shp[0])
            return flat.rearrange("p (k t n) -> p k t n", k=shp[0], t=shp[1])

        def wnext(hold=0):
            c = wstate["c"]
            while wstate["i"] < min(WTOT, c + 3 - hold):
                i = wstate["i"]
                src, shp, T_src = plan[i % NP]
                sl = i % 3
                if len(shp) == 2:
                    dma("sp", wview(sl, shp), src, [T_src], [T_wslot[sl]], T_wslot[sl])
                else:
                    for t_ in range(shp[1]):
                        dma("sp", wview(sl, shp)[:, :, t_, :], src[:, :, t_, :], [T_src], [T_wslot[sl]],
                            T_wslot[sl])
                wstate["i"] += 1
            wstate["c"] += 1
            src, shp, _ = plan[c % NP]
            return wview(c % 3, shp), T_wslot[c % 3]

        kvseq = []
        for G in range(NG):
            for m in range(2):
                for h in range(8):
                    for grp in range(G + 1):
                        kvseq.append((m, h, grp))
        kstate = {"i": 0, "c": 0}
        vstate = {"i": 0, "c": 0}

        def knext():
            c = kstate["c"]
            while kstate["i"] < min(len(kvseq), c + 3):
                i = kstate["i"]
                m, h, grp = kvseq[i]
                sl = i % 3
                kt = KaT if m == 0 else KnT
                aux = onehot if m == 0 else KrT
                dma("pool", ksl[sl][0:64, :], kt[h, :, grp * 1024:(grp + 1) * 1024], [T_scr], [T_ks[sl]], T_ks[sl])
                dma("pool", ksl[sl][64:96, :], aux[:, grp * 1024:(grp + 1) * 1024], [T_scr], [T_ks[sl]], T_ks[sl])
                kstate["i"] += 1
            kstate["c"] += 1
            return c % 3

        def vnext():
            c = vstate["c"]
            while vstate["i"] < min(len(kvseq), c + 3):
                i = vstate["i"]
                m, h, grp = kvseq[i]
                sl = i % 3
                vv = Vas if m == 0 else Vms
                dma("pool", vsl[sl][:], vv[h, grp], [T_scr], [T_vs[sl]], T_vs[sl])
                vstate["i"] += 1
            vstate["c"] += 1
            return c % 3

        bank_rr = {"i": 0}
        banks = [(pA[:, 0:512], T_pA[0]), (pA[:, 512:1024], T_pA[1]), (pA[:, 1024:1536], T_pA[2]), (pF[:], T_pF)]

        def bank():
            b_ = banks[bank_rr["i"] % 4]
            bank_rr["i"] += 1
            return b_

        def norm_sq(xt, T_x):
            act(hbuf[:].rearrange("p k n -> p (k n)"), xt[:].rearrange("p k n -> p (k n)"), AF.Square,
                [T_x], [T_h])
            for k in range(KC):
                mm(pB[:], ones[:], hbuf[:, k, :], k == 0, k == KC - 1, [T_h, T_ones], [T_pB])
            rsqrt_chain(rstd[:], pB[:], 1.0 / D, [T_pB], T_rstd, T_rtmp, rtmp[:])

        def load_x(G_):
            xt_, T_x_ = xs[(NT + G_) % 2], T_xs[(NT + G_) % 2]
            dma("sp", xt_[:], xT_own.rearrange("(k p) n -> p k n", p=128)[:, :, G_ * 512:G_ * 512 + 512], (),
                [T_x_], T_x_)
            dma("sp", tab[:], tabQ[:, :, G_ * 512:G_ * 512 + 512], (), [T_tab], T_tabq)

        hq, T_hq = hbuf2, T_h2

        pti = 0
        for G in range(NG):
            xt, T_x = xs[(NT + G) % 2], T_xs[(NT + G) % 2]
            q0 = G * 512
            if G == 0:
                load_x(0)
                make_hT(xt, T_x, 0, hq, T_hq)
            dma("pool", pt_bf[:], pT_own.rearrange("(k p) n -> p k n", p=128)[:, :, q0:q0 + 512], (), [T_ptbf],
                T_ptbf)
            mark('G%d_q' % G)
            Wcq, T_Wcq = wnext()
            pcs = []
            for kk in range(2):
                pc, T_pc = gbank()
                for k in range(KC):
                    mm(pc, Wcq[:, k, kk * 128:(kk + 1) * 128], hq[:, k, :], k == 0, k == KC - 1,
                       [T_hq, T_Wcq], [T_pc])
                act(sqc[:, kk, :], pc, AF.Square, [T_pc], [T_sqc])
                pcs.append((pc, T_pc))
            for kk in range(2):
                mm(pB[:], ones[:], sqc[:, kk, :], kk == 0, kk == 1, [T_sqc, T_ones], [T_pB])
            rsqrt_chain(rstd[:], pB[:], 1.0 / 256, [T_pB], T_rstd, T_rtmp, rtmp[:])
            for kk in range(2):
                stt(cnT[:, kk, :], pcs[kk][0], gains[:, 33 + kk:34 + kk], rstd[:], ALU.mult, ALU.mult,
                    [pcs[kk][1], T_rstd, T_const], [T_cnT])
            Wqa, T_Wqa = wnext()
            Wqs, T_Wqs = wnext(1)
            for h in range(8):
                p1, T_p1 = gbank()
                for k in range(KC):
                    mm(p1[0:64, :], Wqa[:, k, h * 64:(h + 1) * 64], hq[:, k, :], k == 0, k == KC - 1,
                       [T_hq, T_Wqa], [T_p1])
                p2, T_p2 = gbank()
                for k in range(KC):
                    mm(p2[0:64, :], Wqs[:, k, h * 64:(h + 1) * 64], hq[:, k, :], k == 0, k == KC - 1,
                       [T_hq, T_Wqs], [T_p2])
                rope_fm(p1[0:64, :], p2[0:64, :], tab[0:64, 0, :], tab[0:64, 1, :], Qa[0:64, h, :], T_p1, T_p2, T_Qa)
            Wqb, T_Wqb = wnext()
            for h in range(8):
                p1, T_p1 = gbank()
                for kk in range(2):
                    mm(p1[0:96, :], Wqb[:, kk, h * 96:(h + 1) * 96], cnT[:, kk, :], kk == 0, kk == 1,
                       [T_cnT, T_Wqb], [T_p1])
                p2, T_p2 = gbank()
                for kk in range(2):
                    mm(p2[0:96, :], Wqb[:, kk, 768 + h * 96:768 + (h + 1) * 96], cnT[:, kk, :], kk == 0, kk == 1,
                       [T_cnT, T_Wqb], [T_p2])
                cp("act", Qm[0:64, h, :], p1[0:64, :], [T_p1], [T_Qm])
                rope_fm(p1[64:96, :], p2[64:96, :], tab[64:96, 2, :], tab[64:96, 3, :], Qm[64:96, h, :],
                        T_p1, T_p2, T_Qm)
            if G + 1 < NG:
                load_x(G + 1)
            mark('G%d_gates' % G)
            scb = []
            for s in range(4):
                tk = slice(s * 128, (s + 1) * 128)
                cur = 4 * G + s
                if cur > 0:
                    pX, T_pX = ((pB, T_pB) if s < 2 else (pD, T_pD))
                    pX = pX[:, (s % 2) * 256:(s % 2) * 256 + 256]
                    for h in range(8):
                        mm(pX[:, h * 32:(h + 1) * 32], Qa[0:64, h, tk], kmean[:, h, :], True, True,
                           [T_Qa, T_kmean], [T_pX])
                    scb.append((pX, T_pX))
                else:
                    scb.append(None)
            for j in range(4):
                s = j
                tk = slice(s * 128, (s + 1) * 128)
                cur = 4 * G + s
                if cur > 0:
                    pX, T_pX = scb[s]
                    cp("dve", gate_sb[:, :, 0:cur], pX.rearrange("p (h n) -> p h n", h=8)[:, :, 0:cur],
                       [T_pX], [T_gsb])
                    for h in range(8):
                        sch.add("dve", lambda e, h=h, w=max(cur, 8): e.max(out=top8[:, h, :], in_=gate_sb[:, h, 0:w]),
                                [T_gsb], [T_top8])
                    tt("dve", self_[:, :, 0:cur], gate_sb[:, :, 0:cur], top8[:, :, 2:3].to_broadcast([128, 8, cur]),
                       ALU.is_ge, [T_gsb, T_top8], [T_self])
                    ts("dve", selpad[:, :, 64:64 + cur], self_[:, :, 0:cur], 1.0, -NEG, ALU.subtract, ALU.mult,
                       [T_self], [T_selpad])
                W, T_W = wnext()
                for j4 in range(4):
                    oc = j * 4 + j4
                    pY, T_pY = bank()
                    for k in range(KC):
                        mm(pY, W[:, k, j4 * 128:(j4 + 1) * 128], hq[:, k, :], k == 0, k == KC - 1,
                           [T_hq, T_W], [T_pY])
                    act(sig[:, oc, :], pY, AF.Sigmoid, [T_pY], [T_sig])
                if cur > 0:
                    for hh in range(2):
                        for h4 in range(4):
                            mm(pE[:, h4 * 128:(h4 + 1) * 128], selpad[:, hh * 4 + h4, :], ident[:], True, True,
                               [T_selpad, T_const], [T_pE])
                        cp("dve", Qa[64:96, hh * 4:hh * 4 + 4, tk],
                           pE[64:96, :].rearrange("p (h n) -> p h n", h=4), [T_pE], [T_Qa])
            mark('G%d_attn' % G)
            nk = 8 * G + 8
            LOOK = 2
            steps = [(m, h, kc) for m in range(2) for h in range(8) for kc in range(nk)]
            info = {}

            def emit_qk(i):
                m, h, kc = steps[i]
                Qx, T_Qx = (Qa, T_Qa) if m == 0 else (Qm, T_Qm)
                SC = SC_MOBA if m == 0 else SC_MLA
                kl = kc % 8
                if kl == 0:
                    info["sl"] = knext()
                sl = info["sl"]
                j = kc - 8 * G
                s0 = max(0, j // 2) if j >= 0 else 0
                c0 = s0 * 128
                nonlocal_pti = info.get("pti", 0)
                info["pti"] = nonlocal_pti + 1
                pS, T_pS = banks[nonlocal_pti % 3]
                Pt, T_Pt = PT[nonlocal_pti % 3], T_PT[nonlocal_pti % 3]
                mm(pS[:, c0:512], ksl[sl][:, kl * 128:(kl + 1) * 128], Qx[:, h, c0:512], True, j < 0,
                   [T_ks[sl], T_Qx], [T_pS])
                if j >= 0:
                    mm(pS[:, c0:512], ident[:], diag[:, j, c0:512], False, True, [T_const], [T_pS])
                act(Pt[:, c0:512], pS[:, c0:512], AF.Exp, [T_pS], [T_Pt], scale=SC)
                info[i] = (kl, s0, Pt, T_Pt)

            def emit_pv(i):
                m, h, kc = steps[i]
                kl, s0, Pt, T_Pt = info.pop(i)
                if kl == 0:
                    info["vsl"] = vnext()
                sl = info["vsl"]
                pO, T_pO = (pD, T_pD) if h % 2 == 0 else (pE, T_pE)
                if kc == 0:
                    sch.add("pe", lambda e, pO=pO: e.matmul(pO[:, 0:260], zeros[:, 0:128], zeros[:, 0:260],
                                                            start=True, stop=False, skip_group_check=True),
                            [T_ones], [T_pO])
                for s in range(s0, 4):
                    last = (kc == nk - 1) and (s == 3)
                    sch.add("pe", lambda e, pO=pO, s=s, Pt=Pt, sl=sl, kl=kl, last=last: e.matmul(
                        pO[:, s * 65:(s + 1) * 65], Pt[:, s * 128:(s + 1) * 128], vsl[sl][:, kl, :],
                        start=False, stop=last, skip_group_check=True), [T_Pt, T_vs[sl]], [T_pO])
                if kc == nk - 1:
                    pOv = pO[:, 0:260].rearrange("p (s d) -> p s d", s=4)
                    smv = sm[:, 4:8] if h % 2 == 0 else sm[:, 8:12]
                    T_smv = T_sm4[h % 2]
                    recip(smv.unsqueeze(2), pOv[:, :, 64:65], [T_pO], [T_smv])
                    tt("dve", attn_tok[m][:, :, h * 64:(h + 1) * 64], pOv[:, :, 0:64],
                       smv.unsqueeze(2).to_broadcast([128, 4, 64]), ALU.mult, [T_pO, T_smv], [T_atok[m]])
                    if h == 7:
                        for rnd in range(2):
                            for pp in range(2):
                                pr = rnd * 2 + pp
                                for s in range(4):
                                    tp(pC[:, (pp * 4 + s) * 128:(pp * 4 + s + 1) * 128],
                                       attn_tok[m][:, s, pr * 128:(pr + 1) * 128], [T_atok[m]], [T_pC])
                            cp("dve", attnT[:, m, rnd * 2:rnd * 2 + 2, :],
                               pC[:].rearrange("p (a n) -> p a n", a=2), [T_pC], [T_attnT])

            for i in range(len(steps) + LOOK):
                if i < len(steps):
                    emit_qk(i)
                if i >= LOOK:
                    emit_pv(i - LOOK)
            if dbg and G == 0:
                for nm_, ap_, tl_ in (("d_attnT", attnT, T_attnT), ("d_Qa", Qa, T_Qa), ("d_Qm", Qm, T_Qm),
                                      ("d_sig", sig, T_sig)):
                    dd = dram(nm_, list(ap_.shape), BF16, kind="ExternalOutput")
                    dma("sp", dd, ap_[:], [tl_], [T_out], tl_)
            mark('G%d_dense' % G)
            for jp in range(2):
                WA, T_WA = wnext()
                WB, T_WB = wnext(1)
                for j4 in range(4):
                    oc = jp * 4 + j4
                    pY, T_pY = bank()
                    for pr in range(4):
                        mm(pY, WA[:, pr, j4 * 128:(j4 + 1) * 128], attnT[:, 0, pr, :], pr == 0, pr == 3,
                           [T_attnT, T_WA], [T_pY])
                    tt("dve", tmpf[:], pY, sig[:, oc, :], ALU.mult, [T_pY, T_sig], [T_tmpf])
                    pZ, T_pZ = bank()
                    for pr in range(4):
                        mm(pZ, WB[:, pr, j4 * 128:(j4 + 1) * 128], attnT[:, 1, pr, :], pr == 0, pr == 3,
                           [T_attnT, T_WB], [T_pZ])
                    tt("dve", tmp2[:], pZ, sig[:, 8 + oc, :], ALU.mult, [T_pZ, T_sig], [T_tmp2])
                    tt("dve", hbuf[:, oc, :], tmpf[:], tmp2[:], ALU.add, [T_tmpf, T_tmp2], [T_h])
            mark('G%d_wo' % G)
            for jp in range(2):
                W, T_W = wnext()
                for j4 in range(4):
                    oc = jp * 4 + j4
                    pY, T_pY = bank()
                    for k in range(KC):
                        mm(pY, W[:, k, j4 * 128:(j4 + 1) * 128], hbuf[:, k, :], k == 0, k == KC - 1,
                           [T_h, T_W], [T_pY])
                    tt("dve", xt[:, oc, :], pY, xt[:, oc, :], ALU.add, [T_pY, T_x], [T_x])
            mark('G%d_ffnnorm' % G)
            make_hT(xt, T_x, 8)
            mark('G%d_gu' % G)
            for pn in range(11):
                W, T_W = wnext()
                for j2 in range(2):
                    jb = pn * 2 + j2
                    pG, T_pG = bank()
                    for k in range(KC):
                        mm(pG, W[:, k, 0, j2 * 128:(j2 + 1) * 128], hbuf[:, k, :], k == 0, k == KC - 1,
                           [T_h, T_W], [T_pG])
                    pU, T_pU = bank()
                    for k in range(KC):
                        mm(pU, W[:, k, 1, j2 * 128:(j2 + 1) * 128], hbuf[:, k, :], k == 0, k == KC - 1,
                           [T_h, T_W], [T_pU])
                    tf, T_tf = (tmpf, T_tmpf) if jb % 2 == 0 else (tmp2, T_tmp2)
                    act(tf[:], pG, AF.Silu, [T_pG], [T_tf])
                    tt("dve", actT[:, jb, :], pU, tf[:], ALU.mult, [T_pU, T_tf],
                       [T_actT] + ([T_kast, T_knst, T_vast, T_vmst, T_krst] if G == 0 else []))
            if G + 1 < NG:
                make_hT(xs[(NT + G + 1) % 2], T_xs[(NT + G + 1) % 2], 0, hq, T_hq)
            mark('G%d_down' % G)
            R4 = [banks[0], banks[1], banks[2], (pD[:], T_pD)]
            for cpn in range(2):
                for kg in range(3):
                    W, T_W = wnext()
                    nkg = min(22, kg * 8 + 8) - kg * 8
                    for j4 in range(4):
                        for kk in range(nkg):
                            mm(R4[j4][0], W[:, kk, j4 * 128:(j4 + 1) * 128], actT[:, kg * 8 + kk, :],
                               kg == 0 and kk == 0, kg == 2 and kk == nkg - 1, [T_actT, T_W], [R4[j4][1]])
                for j4 in range(4):
                    oc = cpn * 4 + j4
                    tt("dve", xt[:, oc, :], R4[j4][0], xt[:, oc, :], ALU.add, [R4[j4][1], T_x], [T_x])
            mark('G%d_plenorm' % G)
            make_hT(xt, T_x, 16)
            mark('G%d_ple' % G)
            for jp in range(2):
                W, T_W = wnext()
                for j4 in range(4):
                    oc = jp * 4 + j4
                    pY, T_pY = bank()
                    for k in range(KC):
                        mm(pY, W[:, k, j4 * 128:(j4 + 1) * 128], hbuf[:, k, :], k == 0, k == KC - 1,
                           [T_h, T_W], [T_pY])
                    pZ, T_pZ = bank()
                    for kk in range(2):
                        mm(pZ, Wpp[:, kk, oc * 128:(oc + 1) * 128], pt_bf[:, kk, :], kk == 0, kk == 1,
                           [T_ptbf, T_Wpp], [T_pZ])
                    act(tmpf[:], pY, AF.Sigmoid, [T_pY], [T_tmpf])
                    tt("dve", tmp2[:], pZ, tmpf[:], ALU.mult, [T_pZ, T_tmpf], [T_tmp2])
                    tt("dve", xt[:, oc, :], xt[:, oc, :], tmp2[:], ALU.add, [T_x, T_tmp2], [T_x])
            mark('G%d_final' % G)
            norm_sq(xt, T_x)
            for k in range(KC):
                stt(xt[:, k, :], xt[:, k, :], gains[:, 24 + k:25 + k], rstd[:], ALU.mult, ALU.mult,
                    [T_x, T_rstd, T_const], [T_x])
            dma("pool", outT.rearrange("(k p) n -> p k n", p=128)[:, :, q0:q0 + 512], xt[:], [T_x], [T_out],
                T_xst[(NT + G) % 2])

    mark('end')
    nc._marks = marks
    fin = st.enter_context(nc.semaphore("fin"))
    blk = st.enter_context(nc.Block())
    emap = {"sp": blk.sync, "act": blk.scalar, "dve": blk.vector, "pool": blk.gpsimd, "pe": blk.tensor}

    def handles(run):
        for e in Sched.ENGS:
            def f(h, e=e):
                run(e, h)
            emap[e](f)

    with nc.allow_low_precision(reason="bf16 matmul operands, fp32 accumulation"):
        sch.emit(nc, st, handles)
    st.close()
    return nc


def _rope_tab(S, d):
    half = d // 2
    inv = (1.0 / (np.float32(10000.0) ** (np.arange(half, dtype=np.float32) * np.float32(2.0 / d)))).astype(np.float32)
    ang = (np.arange(S, dtype=np.float32)[:, None] * inv[None, :]).astype(np.float32)
    return np.cos(ang).astype(np.float32), np.sin(ang).astype(np.float32)


def own_index(S, r):
    idx = []
    for G in range(S // 1024):
        for o in OFFS[r]:
            c = 8 * G + o
            idx.append(np.arange(c * 128, (c + 1) * 128))
    return np.concatenate(idx)


def prep(inputs, S, B):
    bf = ml_dtypes.bfloat16
    f32 = np.float32
    x = np.asarray(inputs["x"], f32)
    p = np.asarray(inputs["p"], f32)[0]
    w_in = np.asarray(inputs["w_in"], f32)[0]
    c64, s64 = _rope_tab(S, 64)
    c32, s32 = _rope_tab(S, 32)
    pp = np.arange(128)
    f64 = (pp % 64) % 32
    sg64 = np.where((pp % 64) < 32, -1.0, 1.0).astype(f32)
    f32i = (pp % 32) % 16
    sg32 = np.where((pp % 32) < 16, -1.0, 1.0).astype(f32)
    tabK = np.empty((128, 4, S), f32)
    tabK[:, 0] = c64[:, f64].T
    tabK[:, 1] = s64[:, f64].T
    tabK[:, 2] = c32[:, f32i].T
    tabK[:, 3] = s32[:, f32i].T
    neg64 = sg64 < 0
    neg32 = sg32 < 0
    tabK[neg64, 1] = -tabK[neg64, 1]
    tabK[neg32, 3] = -tabK[neg32, 3]
    onehot = np.zeros((32, S), bf)
    for n in range(S // 256):
        onehot[n, n * 256:(n + 1) * 256] = 1
    ident = np.eye(128).astype(bf)
    gains = np.concatenate([np.asarray(inputs[k], f32).reshape(8, 128).T for k in
                            ("attn_norm", "ffn_norm", "ple_norm", "final_norm")] +
                           [np.asarray(inputs["mla_kv_norm"], f32).reshape(1, 128).T,
                            np.asarray(inputs["mla_q_norm"], f32).reshape(2, 128).T], axis=1)

    def swap_halves(w, hd):
        k_, n_ = w.shape
        v = w.reshape(k_, n_ // hd, 2, hd // 2)
        return np.ascontiguousarray(v[:, :, ::-1, :]).reshape(k_, n_)

    w_qa, w_ka, w_va = w_in[:, 0:512], w_in[:, 512:1024], w_in[:, 1024:1536]
    w_cq, w_ckv, w_kr = w_in[:, 1536:1792], w_in[:, 1792:1920], w_in[:, 1920:1952]
    w_qb0 = np.asarray(inputs["w_q_b"], f32)[0]
    qb3 = w_qb0.reshape(256, 8, 96)
    qb_sw = np.concatenate([qb3[:, :, 0:64], qb3[:, :, 80:96], qb3[:, :, 64:80]], axis=2).reshape(256, 768)
    w_kvb = np.asarray(inputs["w_kv_b"], f32)[0].reshape(128, 8, 128)
    common = {
        "tabK": tabK, "onehot": onehot, "ident": ident,
        "gains": np.ascontiguousarray(gains),
        "w_kside": np.ascontiguousarray(np.concatenate(
            [w_ka, swap_halves(w_ka, 64), w_va, w_ckv, w_kr, swap_halves(w_kr, 32)], axis=1)),
        "w_qside": np.ascontiguousarray(np.concatenate([w_qa, swap_halves(w_qa, 64), w_cq], axis=1)),
        "w_gates": np.ascontiguousarray(w_in[:, 1952:4000]),
        "w_kvb_k": np.ascontiguousarray(w_kvb[:, :, 0:64].reshape(128, 512)),
        "w_kvb_v": np.ascontiguousarray(w_kvb[:, :, 64:128].reshape(128, 512)),
        "w_qb": np.ascontiguousarray(np.concatenate([w_qb0, qb_sw], axis=1)),
        "w_ba": np.asarray(inputs["w_moba_branch"], f32)[0],
        "w_bb": np.asarray(inputs["w_mla_branch"], f32)[0],
        "w_o": np.asarray(inputs["w_o"], f32)[0],
        "w_gu": np.asarray(inputs["w_gate_up"], f32)[0],
        "w_dn": np.asarray(inputs["w_down"], f32)[0],
        "w_pg": np.asarray(inputs["w_ple_gate"], f32)[0],
        "w_pp": np.asarray(inputs["w_ple_proj"], f32)[0],
    }
    diags = []
    kk = np.arange(128)[:, None]
    qq = np.arange(128)[None, :]
    tri = np.where(kk <= qq, 0.0, NEG).astype(np.float32)
    for r in range(2):
        dg = np.zeros((128, 8, 512), np.float32)
        for j in range(8):
            for s, o in enumerate(OFFS[r]):
                blk = tri if j == o else (np.zeros((128, 128), np.float32) if j < o else
                                          np.full((128, 128), NEG, np.float32))
                dg[:, j, s * 128:(s + 1) * 128] = blk
        diags.append(dg.astype(bf))
    in_maps = []
    for b in range(B):
        xT_all = np.ascontiguousarray(x[b].T)
        for r in range(2):
            oi = own_index(S, r)
            m = dict(common)
            m["xT_all"] = xT_all
            m["xT_own"] = np.ascontiguousarray(x[b][oi].T)
            m["pT_own"] = np.ascontiguousarray(p[b][oi].T)
            m["tabQ"] = np.ascontiguousarray(tabK[:, :, oi])
            m["diag"] = diags[r]
            in_maps.append(m)
    return in_maps


_NC_CACHE = {}


def kernel(**inputs):
    x = np.asarray(inputs["x"])
    B, S, _ = x.shape
    in_maps = prep(inputs, S, B)
    key = (S,)
    if key not in _NC_CACHE:
        _NC_CACHE[key] = build(S)
    nc = _NC_CACHE[key]
    res = run_bass_kernel_spmd(nc, in_maps, core_ids=list(range(2 * B)))
    out = np.empty((B, S, D), np.float32)
    i = 0
    for b in range(B):
        for r in range(2):
            oi = own_index(S, r)
            out[b, oi, :] = np.asarray(res.results[i]["outT"]).T
            i += 1
    return out
```

```python
import numpy as np
import ml_dtypes
from contextlib import ExitStack
import concourse.bass as bass
import concourse.mybir as mybir
from concourse.bass_utils import run_bass_kernel_spmd

F32 = mybir.dt.float32
BF16 = mybir.dt.bfloat16
AF = mybir.ActivationFunctionType
ALU = mybir.AluOpType
AX = mybir.AxisListType

D = 1024
KC = 8
DFF = 2816
NEG = -30000.0
EPS = 1e-6
OFFS = ([0, 3, 4, 7], [1, 2, 5, 6])
SC_MOBA = 64 ** -0.5
SC_MLA = 96 ** -0.5


class T:
    __slots__ = ("name", "w", "r", "multi", "sem", "semcnt")

    def __init__(self, name, multi=False):
        self.name = name
        self.w = []
        self.r = []
        self.multi = multi
        self.sem = None
        self.semcnt = 0


class Op:
    __slots__ = ("eng", "fn", "deps", "sig", "dma", "owner", "sigval", "idx")

    def __init__(self, eng, fn, dma, owner):
        self.eng = eng
        self.fn = fn
        self.deps = {}
        self.sig = False
        self.dma = dma
        self.owner = owner
        self.sigval = None


class Sched:
    ENGS = ("pe", "act", "dve", "pool", "sp")

    def __init__(self):
        self.ops = []

    def _dep(self, op, p, raw):
        if p is op:
            return
        if (not p.dma) and (not op.dma) and p.eng == op.eng:
            if p.eng == "pe" or (not raw and p.eng != "pool"):
                return
        op.deps[id(p)] = p

    def add(self, eng, fn, reads=(), writes=(), dma=False, owner=None):
        op = Op(eng, fn, dma, owner)
        for t in reads:
            for w in t.w:
                self._dep(op, w, True)
        for t in writes:
            if not t.multi:
                for w in t.w:
                    if dma and w.dma and w.owner is owner:
                        continue
                    self._dep(op, w, False)
            for r in t.r:
                self._dep(op, r, False)
        for t in reads:
            if not dma:
                t.r = [x for x in t.r if x.dma or x.eng != eng]
            t.r.append(op)
        for t in writes:
            if t.multi:
                t.w.append(op)
            else:
                t.w = [op]
                t.r = []
        self.ops.append(op)
        return op

    def emit(self, nc, stack, handles_fn):
        for op in self.ops:
            for p in op.deps.values():
                p.sig = True
        sems = {}
        for e in self.ENGS:
            sems[e] = stack.enter_context(nc.semaphore("s_" + e))
        cnt = {e: 0 for e in self.ENGS}
        for op in self.ops:
            if op.dma:
                ow = op.owner
                if ow.sem is None:
                    ow.sem = stack.enter_context(nc.semaphore("d_" + ow.name))
                ow.semcnt += 16
                op.sigval = (ow.sem, ow.semcnt)
            elif op.sig:
                cnt[op.eng] += 1
                op.sigval = (sems[op.eng], cnt[op.eng])
        per = {e: [o for o in self.ops if o.eng == e] for e in self.ENGS}
        fw = {}
        for op in self.ops:
            if op.dma:
                fw[id(op.owner)] = (op.owner.sem, op.owner.semcnt)
        self.final_waits = list(fw.values())

        def run(e, h):
            waited = {}
            for op in per[e]:
                need = {}
                for p in op.deps.values():
                    s, v = p.sigval
                    k = id(s)
                    if k not in need or need[k][1] < v:
                        need[k] = (s, v)
                for k, (s, v) in need.items():
                    if waited.get(k, 0) >= v:
                        continue
                    h.wait_ge(s, v)
                    waited[k] = v
                ins = op.fn(h)
                if op.dma:
                    ins.then_inc(op.sigval[0], 16)
                elif op.sig:
                    ins.then_inc(op.sigval[0], 1)
            if e == "sp":
                for (s, v) in self.final_waits:
                    h.wait_ge(s, v)

        handles_fn(run)


def build(S, dbg=False, phases=("A", "M")):
    NT = S // 512
    NG = S // 1024
    SO = S // 2
    NB = 32
    nc = bass.Bass("TRN2", target_bir_lowering=False)
    sch = Sched()
    marks = []

    def mark(name):
        marks.append((name, {e: sum(1 for o in sch.ops if o.eng == e) for e in Sched.ENGS}))
    st = ExitStack()

    def dram(name, shape, dt, kind="ExternalInput"):
        return nc.dram_tensor(name, list(shape), dt, kind=kind).ap()

    xT_all = dram("xT_all", [D, S], F32)
    xT_own = dram("xT_own", [D, SO], F32)
    pT_own = dram("pT_own", [256, SO], F32)
    tabK = dram("tabK", [128, 4, S], F32)
    tabQ = dram("tabQ", [128, 4, SO], F32)
    onehot = dram("onehot", [NB, S], BF16)
    diag_d = dram("diag", [128, 8, 512], BF16)
    ident_d = dram("ident", [128, 128], BF16)
    gains_d = dram("gains", [128, 35], F32)
    w_kside = dram("w_kside", [D, 1728], F32)
    w_qside = dram("w_qside", [D, 1280], F32)
    w_gates = dram("w_gates", [D, 2048], F32)
    w_kvb_k = dram("w_kvb_k", [128, 512], F32)
    w_kvb_v = dram("w_kvb_v", [128, 512], F32)
    w_qb = dram("w_qb", [256, 1536], F32)
    w_ba = dram("w_ba", [512, D], F32)
    w_bb = dram("w_bb", [512, D], F32)
    w_o = dram("w_o", [D, D], F32)
    w_gu = dram("w_gu", [D, 2 * DFF], F32)
    w_dn = dram("w_dn", [DFF, D], F32)
    w_pg = dram("w_pg", [D, D], F32)
    w_pp = dram("w_pp", [256, D], F32)
    outT = dram("outT", [D, SO], F32, kind="ExternalOutput")

    SK = "ExternalOutput" if dbg else "Internal"
    KaT = dram("KaT", [8, 64, S], BF16, kind=SK)
    KnT = dram("KnT", [8, 64, S], BF16, kind=SK)
    KrT = dram("KrT", [32, S], BF16, kind=SK)
    Vas = dram("Vas", [8, S // 1024, 128, 8, 65], BF16, kind=SK)
    Vms = dram("Vms", [8, S // 1024, 128, 8, 65], BF16, kind=SK)
    ksum_d = dram("ksum_d", [128, 4, NB], F32, kind=SK)
    wb = {}
    WLIST = (("w_qside", w_qside), ("w_qb", w_qb), ("w_gates", w_gates), ("w_ba", w_ba), ("w_bb", w_bb),
             ("w_o", w_o), ("w_gu", w_gu), ("w_dn", w_dn), ("w_pg", w_pg), ("w_pp", w_pp))
    for nm, src in WLIST:
        wb[nm] = dram(nm + "_bf", list(src.shape), BF16, kind="Internal")
    dbg_out = {}

    def sb(name, shape, dt):
        return st.enter_context(nc.sbuf_tensor("sb_" + name, list(shape), dt))

    def ps(name, shape, dt):
        return st.enter_context(nc.psum_tensor("ps_" + name, list(shape), dt))

    ident = sb("ident", [128, 128], BF16)
    ones = sb("ones", [128, 128], BF16)
    zeros = sb("zeros", [128, 384], BF16)
    gains = sb("gains", [128, 35], F32)
    diag = sb("diag", [128, 8, 512], BF16)
    wbuf = sb("wbuf", [128, 3 * 4096], BF16)
    WkB = sb("WkB", [128, KC * 192 + 1024], BF16)
    xs = [sb("xs%d" % i, [128, KC, 512], F32) for i in range(2)]
    hbuf = sb("hbuf", [128, KC, 512], BF16)
    hbuf2 = sb("hbuf2", [128, KC, 512], BF16)
    rstd = sb("rstd", [128, 512], F32)
    rtmp = sb("rtmp", [128, 512], F32)
    t1b = [sb("t1b%d" % i, [128, 512], F32) for i in range(2)]
    t2b = [sb("t2b%d" % i, [128, 512], F32) for i in range(2)]
    sqc = sb("sqc", [128, 2, 512], BF16)
    tab = sb("tab", [128, 4, 512], F32)
    shared = sb("shared", [128, 22 * 512], BF16)
    kast = shared[:, 0:2048].rearrange("p (a n) -> p a n", a=4)
    knst = shared[:, 2048:4096].rearrange("p (a n) -> p a n", a=4)
    vast = shared[:, 4096:6176].rearrange("p (h c d) -> p h c d", h=8, c=4)
    vmst = shared[:, 6176:8256].rearrange("p (h c d) -> p h c d", h=8, c=4)
    krst = shared[0:32, 8256:8768]
    sm = sb("sm", [128, 16], F32)
    cnT = sb("cnT", [128, 2, 512], BF16)
    ksum = sb("ksum", [128, 4, NB], F32)
    kmean = sb("kmean", [64, 8, NB], BF16)
    kmean_f = sb("kmean_f", [64, 8, NB], F32)

    T_const = T("const")
    T_ones = T("ones")
    T_wslot = [T("wslot%d" % i) for i in range(3)]
    T_WkB = T("WkB")
    T_wkA = T("wkA")
    T_xs = [T("xs0"), T("xs1")]
    T_h = T("hbuf")
    T_h2 = T("hbuf2")
    T_rstd = T("rstd")
    T_rtmp = T("rtmp")
    T_t1 = [T("t1b0"), T("t1b1")]
    T_t2 = [T("t2b0"), T("t2b1")]
    T_sqc = T("sqc")
    T_tab = T("tab")
    T_tabq = T("tabq")
    T_kast, T_knst, T_vast, T_vmst, T_krst = T("kast"), T("knst"), T("vast"), T("vmst"), T("krst")
    T_sm = T("sm")
    T_cnT = T("cnT")
    T_ksum, T_kmean = T("ksum"), T("kmean")
    T_scr = T("scratch", multi=True)
    T_wscr = T("wscratch", multi=True)
    T_wm = {}
    T_ksd = T("ksum_d")

    pA = ps("pA", [128, 1536], F32)
    pB = ps("pB", [128, 512], F32)
    pC = ps("pC", [128, 1024], BF16)
    pD = ps("pD", [128, 512], F32)
    pE = ps("pE", [128, 512], F32)
    pF = ps("pF", [128, 512], F32)
    T_pA = [T("pA0"), T("pA1"), T("pA2")]
    T_pB, T_pC, T_pD, T_pE, T_pF = T("pB"), T("pC"), T("pD"), T("pE"), T("pF")

    def dma(eng, out, in_, reads, writes, owner):
        sch.add(eng, lambda e: e.dma_start(out=out, in_=in_), reads, writes, dma=True, owner=owner)

    def mm(out, lhsT, rhs, start, stop, reads, writes):
        sch.add("pe", lambda e: e.matmul(out, lhsT, rhs, start=start, stop=stop), reads, writes)

    def tp(out, in_, reads, writes):
        sch.add("pe", lambda e: e.transpose(out, in_, ident[0:in_.shape[0], 0:in_.shape[0]]),
                list(reads) + [T_const], writes)

    def act(out, in_, func, reads, writes, scale=1.0, bias=0.0, accum=None):
        if accum is None:
            sch.add("act", lambda e: e.activation(out, in_, func, bias=bias, scale=scale), reads, writes)
        else:
            sch.add("act", lambda e: e.activation(out, in_, func, bias=bias, scale=scale, accum_out=accum),
                    reads, writes)

    def tt(eng, out, in0, in1, op, reads, writes):
        sch.add(eng, lambda e: e.tensor_tensor(out, in0, in1, op), reads, writes)

    def ts(eng, out, in0, s1, s2, op0, op1, reads, writes):
        if op1 is None:
            sch.add(eng, lambda e: e.tensor_scalar(out, in0, s1, None, op0), reads, writes)
        else:
            sch.add(eng, lambda e: e.tensor_scalar(out, in0, s1, s2, op0, op1), reads, writes)

    def stt(out, in0, scalar, in1, op0, op1, reads, writes):
        sch.add("dve", lambda e: e.scalar_tensor_tensor(out, in0, scalar, in1, op0, op1), reads, writes)

    def cp(eng, out, in_, reads, writes):
        if eng == "act":
            sch.add("act", lambda e: e.copy(out, in_), reads, writes)
        else:
            sch.add(eng, lambda e: e.tensor_copy(out, in_), reads, writes)

    def recip(out, in_, reads, writes):
        sch.add("dve", lambda e: e.reciprocal(out, in_), reads, writes)

    def memset(eng, ap, val, writes):
        sch.add(eng, lambda e: e.memset(ap, val), (), writes)

    def rsqrt_chain(out, in_, n_inv, reads_in, T_out, T_tmp, tmp):
        ts("dve", tmp, in_, n_inv, EPS, ALU.mult, ALU.add, reads_in, [T_tmp])
        act(tmp, tmp, AF.Ln, [T_tmp], [T_tmp])
        act(out, tmp, AF.Exp, [T_tmp], [T_out], scale=-0.5)

    dma("sp", ident[:], ident_d, (), [T_const], T_const)
    dma("sp", gains[:], gains_d, (), [T_const], T_const)
    dma("sp", diag[:], diag_d, (), [T_const], T_const)
    memset("dve", ones[:], 1.0, [T_ones])
    memset("dve", zeros[:], 0.0, [T_ones])
    memset("dve", ksum[:], 0.0, [T_ksum])
    memset("pool", vast, 1.0, [T_vast])
    memset("pool", vmst, 1.0, [T_vmst])
    Wk = wbuf[:, 0:KC * 1536].rearrange("p (k n) -> p k n", k=KC)
    WkC = WkB[:, 0:KC * 192].rearrange("p (k n) -> p k n", k=KC)
    Wkk = WkB[:, KC * 192:KC * 192 + 512]
    Wkv = WkB[:, KC * 192 + 512:KC * 192 + 1024]
    dma("sp", xs[0][:], xT_all.rearrange("(k p) n -> p k n", p=128)[:, :, 0:512], (), [T_xs[0]], T_xs[0])
    dma("pool", tab[:], tabK[:, :, 0:512], (), [T_tab], T_tab)
    wsrc = w_kside.rearrange("(k p) n -> p k n", p=128)
    for k in range(KC):
        dma("pool", WkC[:, k, :], wsrc[:, k, 1536:1728], (), [T_WkB], T_WkB)
    dma("pool", Wkk, w_kvb_k, (), [T_WkB], T_WkB)
    dma("pool", Wkv, w_kvb_v, (), [T_WkB], T_WkB)
    for k in range(KC):
        dma("pool", Wk[:, k, 0:512], wsrc[:, k, 0:512], (), T_wslot, T_wkA)
    stg = xs[1][:].rearrange("p k n -> p (k n)")
    T_stg = T_xs[1]
    for blk in (1, 2):
        for hf in range(2):
            dma("sp", stg[:, hf * 2048:(hf + 1) * 2048].rearrange("p (k n) -> p k n", k=4),
                wsrc[:, hf * 4:hf * 4 + 4, blk * 512:(blk + 1) * 512], (), [T_stg], T_stg)
        cp("dve", Wk[:, :, blk * 512:(blk + 1) * 512], stg.rearrange("p (k n) -> p k n", k=KC), [T_stg], T_wslot)
    T_wall = T_wslot + [T_WkB]
    if "M" in phases:
        first = True
        for nm, src in WLIST:
            rows = src.shape[0]
            T_wm[nm] = T("wscr_" + nm, multi=True)
            for r0 in range(0, rows, 128):
                r1 = min(rows, r0 + 128)
                dma("pool", wb[nm][r0:r1, :], src[r0:r1, :], (T_wslot if first else ()), [T_wm[nm]], T_wscr)
                first = False

    def make_hT(xt, T_x, gcol0, hb=None, T_hb=None):
        if hb is None:
            hb, T_hb = hbuf, T_h
        for hf in range(2):
            act(hb[:, hf * 4:hf * 4 + 4, :].rearrange("p k n -> p (k n)"),
                xt[:, hf * 4:hf * 4 + 4, :].rearrange("p k n -> p (k n)"), AF.Square, [T_x], [T_hb])
            for k in range(hf * 4, hf * 4 + 4):
                mm(pB[:], ones[:], hb[:, k, :], k == 0, k == KC - 1, [T_hb, T_ones], [T_pB])
        rsqrt_chain(rstd[:], pB[:], 1.0 / D, [T_pB], T_rstd, T_rtmp, rtmp[:])
        for k in range(KC):
            stt(hb[:, k, :], xt[:, k, :], gains[:, gcol0 + k:gcol0 + k + 1], rstd[:], ALU.mult, ALU.mult,
                [T_x, T_rstd, T_const], [T_hb])

    rr = {"i": 0, "b": 0}

    def rope_fm(p1, p2, cosT, sinT, out, T_p1, T_p2, T_o):
        i = rr["i"] % 2
        rr["i"] += 1
        np_ = p1.shape[0]
        b0 = p1.base_partition() if hasattr(p1, "base_partition") else 0
        a1 = t1b[i][b0:b0 + np_, :]
        a2 = t2b[i][b0:b0 + np_, :]
        tt("dve", a1, p1, cosT, ALU.mult, [T_p1, T_tab], [T_t1[i]])
        tt("dve", a2, p2, sinT, ALU.mult, [T_p2, T_tab], [T_t2[i]])
        tt("pool", out, a1, a2, ALU.add, [T_t1[i], T_t2[i]], [T_o])

    gbanks = None

    def gbank():
        b_ = gbanks[rr["b"] % len(gbanks)]
        rr["b"] += 1
        return b_

    mark('start')
    gbanks = [(pA[:, 0:512], T_pA[0]), (pA[:, 512:1024], T_pA[1]), (pA[:, 1024:1536], T_pA[2]),
              (pF[:], T_pF), (pD[:], T_pD), (pE[:], T_pE)]
    if "A" in phases:
        def a_load_x(t_):
            dma("sp", xs[t_ % 2][:], xT_all.rearrange("(k p) n -> p k n", p=128)[:, :, t_ * 512:t_ * 512 + 512],
                (), [T_xs[t_ % 2]], T_xs[t_ % 2])

        def a_load_tab(t_):
            dma("pool", tab[:], tabK[:, :, t_ * 512:t_ * 512 + 512], (), [T_tab], T_tab)

        hbs = [(hbuf, T_h), (hbuf2, T_h2)]
        if NT > 1:
            a_load_x(1)
        make_hT(xs[0], T_xs[0], 0, hbuf, T_h)
        for t in range(NT):
            xt, T_x = xs[t % 2], T_xs[t % 2]
            hb, T_hb = hbs[t % 2]
            t0 = t * 512
            pc, T_pc = gbank()
            for k in range(KC):
                mm(pc, WkC[:, k, 0:128], hb[:, k, :], k == 0, k == KC - 1, [T_hb] + T_wall, [T_pc])
            act(sqc[:, 0, :], pc, AF.Square, [T_pc], [T_sqc])
            mm(pB[:], ones[:], sqc[:, 0, :], True, True, [T_sqc, T_ones], [T_pB])
            rsqrt_chain(rstd[:], pB[:], 1.0 / 128, [T_pB], T_rstd, T_rtmp, rtmp[:])
            stt(cnT[:, 0, :], pc, gains[:, 32:33], rstd[:], ALU.mult, ALU.mult, [T_pc, T_rstd, T_const], [T_cnT])
            for pr in range(4):
                p1, T_p1 = gbank()
                for k in range(KC):
                    mm(p1, Wk[:, k, pr * 128:(pr + 1) * 128], hb[:, k, :], k == 0, k == KC - 1,
                       [T_hb] + T_wall, [T_p1])
                p2, T_p2 = gbank()
                for k in range(KC):
                    mm(p2, Wk[:, k, 512 + pr * 128:512 + (pr + 1) * 128], hb[:, k, :], k == 0, k == KC - 1,
                       [T_hb] + T_wall, [T_p2])
                rope_fm(p1, p2, tab[:, 0, :], tab[:, 1, :], kast[:, pr, :], T_p1, T_p2, T_kast)
            if t + 1 < NT:
                make_hT(xs[(t + 1) % 2], T_xs[(t + 1) % 2], 0, hbs[(t + 1) % 2][0], hbs[(t + 1) % 2][1])
            for s_ in range(4):
                tk = slice(s_ * 128, (s_ + 1) * 128)
                pv, T_pv = gbank()
                for k in range(KC):
                    mm(pv, hb[:, k, tk], Wk[:, k, 1024:1536], k == 0, k == KC - 1, [T_hb] + T_wall, [T_pv])
                cp("act", vast[:, :, s_, 0:64], pv.rearrange("p (h d) -> p h d", h=8), [T_pv], [T_vast])
            for pr in range(4):
                pk, T_pk = gbank()
                mm(pk, Wkk[:, pr * 128:(pr + 1) * 128], cnT[:, 0, :], True, True, [T_cnT] + T_wall, [T_pk])
                cp("act", knst[:, pr, :], pk, [T_pk], [T_knst])
            for s_ in range(4):
                tk = slice(s_ * 128, (s_ + 1) * 128)
                pv, T_pv = gbank()
                mm(pv, cnT[:, 0, tk], Wkv, True, True, [T_cnT] + T_wall, [T_pv])
                cp("dve", vmst[:, :, s_, 0:64], pv.rearrange("p (h d) -> p h d", h=8), [T_pv], [T_vmst])
            p1, T_p1 = gbank()
            for k in range(KC):
                mm(p1[0:32, :], WkC[:, k, 128:160], hb[:, k, :], k == 0, k == KC - 1, [T_hb] + T_wall, [T_p1])
            p2, T_p2 = gbank()
            for k in range(KC):
                mm(p2[0:32, :], WkC[:, k, 160:192], hb[:, k, :], k == 0, k == KC - 1, [T_hb] + T_wall, [T_p2])
            rope_fm(p1[0:32, :], p2[0:32, :], tab[0:32, 2, :], tab[0:32, 3, :], krst, T_p1, T_p2, T_krst)
            if t + 1 < NT:
                a_load_tab(t + 1)
            if t + 2 < NT:
                a_load_x(t + 2)
            sch.add("dve", lambda e, t=t: e.tensor_reduce(
                ksum[:, :, 2 * t:2 * t + 2], kast.rearrange("p a (b j) -> p a b j", b=2), AX.X, ALU.add),
                [T_kast], [T_ksum])
            dma("sp", KaT.rearrange("(a two) d n -> (two d) a n", two=2)[:, :, t0:t0 + 512], kast,
                [T_kast], [T_scr], T_kast)
            dma("sp", KnT.rearrange("(a two) d n -> (two d) a n", two=2)[:, :, t0:t0 + 512], knst,
                [T_knst], [T_scr], T_knst)
            dma("sp", KrT[:, t0:t0 + 512], krst, [T_krst], [T_scr], T_krst)
            g, hs = t // 2, (t % 2) * 4
            dma("sp", Vas[:, g, :, hs:hs + 4, :].rearrange("h p c d -> p h c d"), vast, [T_vast], [T_scr],
                T_vast)
            dma("sp", Vms[:, g, :, hs:hs + 4, :].rearrange("h p c d -> p h c d"), vmst, [T_vmst], [T_scr],
                T_vmst)
        dma("sp", ksum_d, ksum[:], [T_ksum], [T_ksd], T_ksum)
        dma("sp", kmean_f[:].rearrange("d (a two) n -> d a two n", two=2),
            ksum_d.rearrange("(two d) a n -> d a two n", two=2), [T_ksd], [T_kmean], T_kmean)
        ts("dve", kmean[:], kmean_f[:], 1.0 / 256, None, ALU.mult, None, [T_kmean], [T_kmean])

    if "M" in phases:
        Qa = sb("Qa", [96, 8, 512], BF16)
        Qm = sb("Qm", [96, 8, 512], BF16)
        sig = sb("sig", [128, 16, 512], BF16)
        ksl = [sb("ksl%d" % i, [96, 1024], BF16) for i in range(3)]
        vsl = [sb("vsl%d" % i, [128, 8, 65], BF16) for i in range(3)]
        PT = [sb("PT%d" % i, [128, 512], BF16) for i in range(3)]
        attn_tok = [sb("attn_tok%d" % i, [128, 4, 512], BF16) for i in range(2)]
        attnT = sb("attnT", [128, 2, 4, 512], BF16)
        gate_sb = sb("gate_sb", [128, 8, 32], F32)
        top8 = sb("top8", [128, 8, 8], F32)
        self_ = sb("self", [128, 8, 32], F32)
        selpad = sb("selpad", [128, 8, 128], BF16)
        actT = shared[:, :].rearrange("p (j n) -> p j n", j=22)
        tmp2 = sb("tmp2", [128, 512], F32)
        pt_bf = sb("pt_bf", [128, 2, 512], BF16)
        Wpp = sb("Wpp", [128, 2, 1024], BF16)
        T_Wpp = T("Wpp")
        dma("pool", Wpp[:], w_pp.rearrange("(k p) n -> p k n", p=128), (), [T_Wpp], T_Wpp)
        tmpf = rtmp
        T_tmpf = T_rtmp
        T_Qa, T_Qm, T_sig = T("Qa"), T("Qm"), T("sig")
        T_ks = [T("ks%d" % i) for i in range(3)]
        T_vs = [T("vs%d" % i) for i in range(3)]
        T_PT = [T("PT%d" % i) for i in range(3)]
        T_atok, T_attnT = [T("attn_tok0"), T("attn_tok1")], T("attnT")
        T_sm4 = [T("sm4a"), T("sm4b")]
        T_gsb, T_top8, T_self, T_selpad = T("gate_sb"), T("top8"), T("self"), T("selpad")
        T_actT, T_tmp2, T_ptbf = T("actT"), T("tmp2"), T("pt_bf")
        T_out = T("out", multi=True)
        T_xst = [T("xst0"), T("xst1")]

        memset("pool", gate_sb[:], -1e30, [T_gsb])
        memset("pool", selpad[:], 0.0, [T_selpad])
        memset("pool", Qa[:], 0.0, [T_Qa])

        def v_k(name, k0, k1, c0, c1):
            return wb[name].rearrange("(k p) n -> p k n", p=128)[:, k0:k1, c0:c1], (k1 - k0, c1 - c0), T_wm[name]
        plan = []
        plan.append(v_k("w_qside", 0, 8, 1024, 1280))
        plan.append(v_k("w_qside", 0, 8, 0, 512))
        plan.append(v_k("w_qside", 0, 8, 512, 1024))
        plan.append(v_k("w_qb", 0, 2, 0, 1536))
        for j in range(4):
            plan.append(v_k("w_gates", 0, 8, j * 512, (j + 1) * 512))
        for j in range(2):
            plan.append(v_k("w_ba", 0, 4, j * 512, (j + 1) * 512))
            plan.append(v_k("w_bb", 0, 4, j * 512, (j + 1) * 512))
        for j in range(2):
            plan.append(v_k("w_o", 0, 8, j * 512, (j + 1) * 512))
        for j in range(11):
            plan.append((wb["w_gu"].rearrange("(k p) (t n) -> p k t n", p=128, t=2)[:, :, :, j * 256:(j + 1) * 256],
                         (8, 2, 256), T_wm["w_gu"]))
        for cpn in range(2):
            for kg in range(3):
                plan.append(v_k("w_dn", kg * 8, min(22, kg * 8 + 8), cpn * 512, (cpn + 1) * 512))
        for j in range(2):
            plan.append(v_k("w_pg", 0, 8, j * 512, (j + 1) * 512))
        NP = len(plan)
        wstate = {"i": 0, "c": 0}
        WTOT = NP * NG

        def wview(slot, shp):
            n = 1
            for d_ in shp:
                n *= d_
            flat = wbuf[:, slot * 4096:slot * 4096 + n]
            if len(shp) == 2:
                return flat.rearrange("p (k n) -> p k n", k=shp[0])
            return flat.rearrange("p (k t n) -> p k t n", k=shp[0], t=shp[1])

        def wnext(hold=0):
            c = wstate["c"]
            while wstate["i"] < min(WTOT, c + 3 - hold):
                i = wstate["i"]
                src, shp, T_src = plan[i % NP]
                sl = i % 3
                if len(shp) == 2:
                    dma("sp", wview(sl, shp), src, [T_src], [T_wslot[sl]], T_wslot[sl])
                else:
                    for t_ in range(shp[1]):
                        dma("sp", wview(sl, shp)[:, :, t_, :], src[:, :, t_, :], [T_src], [T_wslot[sl]],
                            T_wslot[sl])
                wstate["i"] += 1
            wstate["c"] += 1
            src, shp, _ = plan[c % NP]
            return wview(c % 3, shp), T_wslot[c % 3]

        kvseq = []
        for G in range(NG):
            for m in range(2):
                for h in range(8):
                    for grp in range(G + 1):
                        kvseq.append((m, h, grp))
        kstate = {"i": 0, "c": 0}
        vstate = {"i": 0, "c": 0}

        def knext():
            c = kstate["c"]
            while kstate["i"] < min(len(kvseq), c + 3):
                i = kstate["i"]
                m, h, grp = kvseq[i]
                sl = i % 3
                kt = KaT if m == 0 else KnT
                aux = onehot if m == 0 else KrT
                dma("pool", ksl[sl][0:64, :], kt[h, :, grp * 1024:(grp + 1) * 1024], [T_scr], [T_ks[sl]], T_ks[sl])
                dma("pool", ksl[sl][64:96, :], aux[:, grp * 1024:(grp + 1) * 1024], [T_scr], [T_ks[sl]], T_ks[sl])
                kstate["i"] += 1
            kstate["c"] += 1
            return c % 3

        def vnext():
            c = vstate["c"]
            while vstate["i"] < min(len(kvseq), c + 3):
                i = vstate["i"]
                m, h, grp = kvseq[i]
                sl = i % 3
                vv = Vas if m == 0 else Vms
                dma("pool", vsl[sl][:], vv[h, grp], [T_scr], [T_vs[sl]], T_vs[sl])
                vstate["i"] += 1
            vstate["c"] += 1
            return c % 3

        bank_rr = {"i": 0}
        banks = [(pA[:, 0:512], T_pA[0]), (pA[:, 512:1024], T_pA[1]), (pA[:, 1024:1536], T_pA[2]), (pF[:], T_pF)]

        def bank():
            b_ = banks[bank_rr["i"] % 4]
            bank_rr["i"] += 1
            return b_

        def norm_sq(xt, T_x):
            act(hbuf[:].rearrange("p k n -> p (k n)"), xt[:].rearrange("p k n -> p (k n)"), AF.Square,
                [T_x], [T_h])
            for k in range(KC):
                mm(pB[:], ones[:], hbuf[:, k, :], k == 0, k == KC - 1, [T_h, T_ones], [T_pB])
            rsqrt_chain(rstd[:], pB[:], 1.0 / D, [T_pB], T_rstd, T_rtmp, rtmp[:])

        def load_x(G_):
            xt_, T_x_ = xs[(NT + G_) % 2], T_xs[(NT + G_) % 2]
            dma("sp", xt_[:], xT_own.rearrange("(k p) n -> p k n", p=128)[:, :, G_ * 512:G_ * 512 + 512], (),
                [T_x_], T_x_)
            dma("sp", tab[:], tabQ[:, :, G_ * 512:G_ * 512 + 512], (), [T_tab], T_tabq)

        hq, T_hq = hbuf2, T_h2

        pti = 0
        for G in range(NG):
            xt, T_x = xs[(NT + G) % 2], T_xs[(NT + G) % 2]
            q0 = G * 512
            if G == 0:
                load_x(0)
                make_hT(xt, T_x, 0, hq, T_hq)
            dma("pool", pt_bf[:], pT_own.rearrange("(k p) n -> p k n", p=128)[:, :, q0:q0 + 512], (), [T_ptbf],
                T_ptbf)
            mark('G%d_q' % G)
            Wcq, T_Wcq = wnext()
            pcs = []
            for kk in range(2):
                pc, T_pc = gbank()
                for k in range(KC):
                    mm(pc, Wcq[:, k, kk * 128:(kk + 1) * 128], hq[:, k, :], k == 0, k == KC - 1,
                       [T_hq, T_Wcq], [T_pc])
                act(sqc[:, kk, :], pc, AF.Square, [T_pc], [T_sqc])
                pcs.append((pc, T_pc))
            for kk in range(2):
                mm(pB[:], ones[:], sqc[:, kk, :], kk == 0, kk == 1, [T_sqc, T_ones], [T_pB])
            rsqrt_chain(rstd[:], pB[:], 1.0 / 256, [T_pB], T_rstd, T_rtmp, rtmp[:])
            for kk in range(2):
                stt(cnT[:, kk, :], pcs[kk][0], gains[:, 33 + kk:34 + kk], rstd[:], ALU.mult, ALU.mult,
                    [pcs[kk][1], T_rstd, T_const], [T_cnT])
            Wqa, T_Wqa = wnext()
            Wqs, T_Wqs = wnext(1)
            for h in range(8):
                p1, T_p1 = gbank()
                for k in range(KC):
                    mm(p1[0:64, :], Wqa[:, k, h * 64:(h + 1) * 64], hq[:, k, :], k == 0, k == KC - 1,
                       [T_hq, T_Wqa], [T_p1])
                p2, T_p2 = gbank()
                for k in range(KC):
                    mm(p2[0:64, :], Wqs[:, k, h * 64:(h + 1) * 64], hq[:, k, :], k == 0, k == KC - 1,
                       [T_hq, T_Wqs], [T_p2])
                rope_fm(p1[0:64, :], p2[0:64, :], tab[0:64, 0, :], tab[0:64, 1, :], Qa[0:64, h, :], T_p1, T_p2, T_Qa)
            Wqb, T_Wqb = wnext()
            for h in range(8):
                p1, T_p1 = gbank()
                for kk in range(2):
                    mm(p1[0:96, :], Wqb[:, kk, h * 96:(h + 1) * 96], cnT[:, kk, :], kk == 0, kk == 1,
                       [T_cnT, T_Wqb], [T_p1])
                p2, T_p2 = gbank()
                for kk in range(2):
                    mm(p2[0:96, :], Wqb[:, kk, 768 + h * 96:768 + (h + 1) * 96], cnT[:, kk, :], kk == 0, kk == 1,
                       [T_cnT, T_Wqb], [T_p2])
                cp("act", Qm[0:64, h, :], p1[0:64, :], [T_p1], [T_Qm])
                rope_fm(p1[64:96, :], p2[64:96, :], tab[64:96, 2, :], tab[64:96, 3, :], Qm[64:96, h, :],
                        T_p1, T_p2, T_Qm)
            if G + 1 < NG:
                load_x(G + 1)
            mark('G%d_gates' % G)
            scb = []
            for s in range(4):
                tk = slice(s * 128, (s + 1) * 128)
                cur = 4 * G + s
                if cur > 0:
                    pX, T_pX = ((pB, T_pB) if s < 2 else (pD, T_pD))
                    pX = pX[:, (s % 2) * 256:(s % 2) * 256 + 256]
                    for h in range(8):
                        mm(pX[:, h * 32:(h + 1) * 32], Qa[0:64, h, tk], kmean[:, h, :], True, True,
                           [T_Qa, T_kmean], [T_pX])
                    scb.append((pX, T_pX))
                else:
                    scb.append(None)
            for j in range(4):
                s = j
                tk = slice(s * 128, (s + 1) * 128)
                cur = 4 * G + s
                if cur > 0:
                    pX, T_pX = scb[s]
                    cp("dve", gate_sb[:, :, 0:cur], pX.rearrange("p (h n) -> p h n", h=8)[:, :, 0:cur],
                       [T_pX], [T_gsb])
                    for h in range(8):
                        sch.add("dve", lambda e, h=h, w=max(cur, 8): e.max(out=top8[:, h, :], in_=gate_sb[:, h, 0:w]),
                                [T_gsb], [T_top8])
                    tt("dve", self_[:, :, 0:cur], gate_sb[:, :, 0:cur], top8[:, :, 2:3].to_broadcast([128, 8, cur]),
                       ALU.is_ge, [T_gsb, T_top8], [T_self])
                    ts("dve", selpad[:, :, 64:64 + cur], self_[:, :, 0:cur], 1.0, -NEG, ALU.subtract, ALU.mult,
                       [T_self], [T_selpad])
                W, T_W = wnext()
                for j4 in range(4):
                    oc = j * 4 + j4
                    pY, T_pY = bank()
                    for k in range(KC):
                        mm(pY, W[:, k, j4 * 128:(j4 + 1) * 128], hq[:, k, :], k == 0, k == KC - 1,
                           [T_hq, T_W], [T_pY])
                    act(sig[:, oc, :], pY, AF.Sigmoid, [T_pY], [T_sig])
                if cur > 0:
                    for hh in range(2):
                        for h4 in range(4):
                            mm(pE[:, h4 * 128:(h4 + 1) * 128], selpad[:, hh * 4 + h4, :], ident[:], True, True,
                               [T_selpad, T_const], [T_pE])
                        cp("dve", Qa[64:96, hh * 4:hh * 4 + 4, tk],
                           pE[64:96, :].rearrange("p (h n) -> p h n", h=4), [T_pE], [T_Qa])
            mark('G%d_attn' % G)
            nk = 8 * G + 8
            LOOK = 2
            steps = [(m, h, kc) for m in range(2) for h in range(8) for kc in range(nk)]
            info = {}

            def emit_qk(i):
                m, h, kc = steps[i]
                Qx, T_Qx = (Qa, T_Qa) if m == 0 else (Qm, T_Qm)
                SC = SC_MOBA if m == 0 else SC_MLA
                kl = kc % 8
                if kl == 0:
                    info["sl"] = knext()
                sl = info["sl"]
                j = kc - 8 * G
                s0 = max(0, j // 2) if j >= 0 else 0
                c0 = s0 * 128
                nonlocal_pti = info.get("pti", 0)
                info["pti"] = nonlocal_pti + 1
                pS, T_pS = banks[nonlocal_pti % 3]
                Pt, T_Pt = PT[nonlocal_pti % 3], T_PT[nonlocal_pti % 3]
                mm(pS[:, c0:512], ksl[sl][:, kl * 128:(kl + 1) * 128], Qx[:, h, c0:512], True, j < 0,
                   [T_ks[sl], T_Qx], [T_pS])
                if j >= 0:
                    mm(pS[:, c0:512], ident[:], diag[:, j, c0:512], False, True, [T_const], [T_pS])
                act(Pt[:, c0:512], pS[:, c0:512], AF.Exp, [T_pS], [T_Pt], scale=SC)
                info[i] = (kl, s0, Pt, T_Pt)

            def emit_pv(i):
                m, h, kc = steps[i]
                kl, s0, Pt, T_Pt = info.pop(i)
                if kl == 0:
                    info["vsl"] = vnext()
                sl = info["vsl"]
                pO, T_pO = (pD, T_pD) if h % 2 == 0 else (pE, T_pE)
                if kc == 0:
                    sch.add("pe", lambda e, pO=pO: e.matmul(pO[:, 0:260], zeros[:, 0:128], zeros[:, 0:260],
                                                            start=True, stop=False, skip_group_check=True),
                            [T_ones], [T_pO])
                for s in range(s0, 4):
                    last = (kc == nk - 1) and (s == 3)
                    sch.add("pe", lambda e, pO=pO, s=s, Pt=Pt, sl=sl, kl=kl, last=last: e.matmul(
                        pO[:, s * 65:(s + 1) * 65], Pt[:, s * 128:(s + 1) * 128], vsl[sl][:, kl, :],
                        start=False, stop=last, skip_group_check=True), [T_Pt, T_vs[sl]], [T_pO])
                if kc == nk - 1:
                    pOv = pO[:, 0:260].rearrange("p (s d) -> p s d", s=4)
                    smv = sm[:, 4:8] if h % 2 == 0 else sm[:, 8:12]
                    T_smv = T_sm4[h % 2]
                    recip(smv.unsqueeze(2), pOv[:, :, 64:65], [T_pO], [T_smv])
                    tt("dve", attn_tok[m][:, :, h * 64:(h + 1) * 64], pOv[:, :, 0:64],
                       smv.unsqueeze(2).to_broadcast([128, 4, 64]), ALU.mult, [T_pO, T_smv], [T_atok[m]])
                    if h == 7:
                        for rnd in range(2):
                            for pp in range(2):
                                pr = rnd * 2 + pp
                                for s in range(4):
                                    tp(pC[:, (pp * 4 + s) * 128:(pp * 4 + s + 1) * 128],
                                       attn_tok[m][:, s, pr * 128:(pr + 1) * 128], [T_atok[m]], [T_pC])
                            cp("dve", attnT[:, m, rnd * 2:rnd * 2 + 2, :],
                               pC[:].rearrange("p (a n) -> p a n", a=2), [T_pC], [T_attnT])

            for i in range(len(steps) + LOOK):
                if i < len(steps):
                    emit_qk(i)
                if i >= LOOK:
                    emit_pv(i - LOOK)
            if dbg and G == 0:
                for nm_, ap_, tl_ in (("d_attnT", attnT, T_attnT), ("d_Qa", Qa, T_Qa), ("d_Qm", Qm, T_Qm),
                                      ("d_sig", sig, T_sig)):
                    dd = dram(nm_, list(ap_.shape), BF16, kind="ExternalOutput")
                    dma("sp", dd, ap_[:], [tl_], [T_out], tl_)
            mark('G%d_dense' % G)
            for jp in range(2):
                WA, T_WA = wnext()
                WB, T_WB = wnext(1)
                for j4 in range(4):
                    oc = jp * 4 + j4
                    pY, T_pY = bank()
                    for pr in range(4):
                        mm(pY, WA[:, pr, j4 * 128:(j4 + 1) * 128], attnT[:, 0, pr, :], pr == 0, pr == 3,
                           [T_attnT, T_WA], [T_pY])
                    ta, T_ta = t1b[oc % 2], T_t1[oc % 2]
                    tb_, T_tb = t2b[oc % 2], T_t2[oc % 2]
                    tt("dve", ta[:], pY, sig[:, oc, :], ALU.mult, [T_pY, T_sig], [T_ta])
                    pZ, T_pZ = bank()
                    for pr in range(4):
                        mm(pZ, WB[:, pr, j4 * 128:(j4 + 1) * 128], attnT[:, 1, pr, :], pr == 0, pr == 3,
                           [T_attnT, T_WB], [T_pZ])
                    tt("dve", tb_[:], pZ, sig[:, 8 + oc, :], ALU.mult, [T_pZ, T_sig], [T_tb])
                    tt("pool", hbuf[:, oc, :], ta[:], tb_[:], ALU.add, [T_ta, T_tb], [T_h])
            mark('G%d_wo' % G)
            for jp in range(2):
                W, T_W = wnext()
                for j4 in range(4):
                    oc = jp * 4 + j4
                    pY, T_pY = bank()
                    for k in range(KC):
                        mm(pY, W[:, k, j4 * 128:(j4 + 1) * 128], hbuf[:, k, :], k == 0, k == KC - 1,
                           [T_h, T_W], [T_pY])
                    tt("dve", xt[:, oc, :], pY, xt[:, oc, :], ALU.add, [T_pY, T_x], [T_x])
            mark('G%d_ffnnorm' % G)
            make_hT(xt, T_x, 8)
            mark('G%d_gu' % G)
            for pn in range(11):
                W, T_W = wnext()
                for j2 in range(2):
                    jb = pn * 2 + j2
                    pG, T_pG = bank()
                    for k in range(KC):
                        mm(pG, W[:, k, 0, j2 * 128:(j2 + 1) * 128], hbuf[:, k, :], k == 0, k == KC - 1,
                           [T_h, T_W], [T_pG])
                    pU, T_pU = bank()
                    for k in range(KC):
                        mm(pU, W[:, k, 1, j2 * 128:(j2 + 1) * 128], hbuf[:, k, :], k == 0, k == KC - 1,
                           [T_h, T_W], [T_pU])
                    tf, T_tf = (tmpf, T_tmpf) if jb % 2 == 0 else (tmp2, T_tmp2)
                    act(tf[:], pG, AF.Silu, [T_pG], [T_tf])
                    tt("dve", actT[:, jb, :], pU, tf[:], ALU.mult, [T_pU, T_tf],
                       [T_actT] + ([T_kast, T_knst, T_vast, T_vmst, T_krst] if G == 0 else []))
            if G + 1 < NG:
                make_hT(xs[(NT + G + 1) % 2], T_xs[(NT + G + 1) % 2], 0, hq, T_hq)
            mark('G%d_down' % G)
            R4 = [banks[0], banks[1], banks[2], (pD[:], T_pD)]
            for cpn in range(2):
                for kg in range(3):
                    W, T_W = wnext()
                    nkg = min(22, kg * 8 + 8) - kg * 8
                    for j4 in range(4):
                        for kk in range(nkg):
                            mm(R4[j4][0], W[:, kk, j4 * 128:(j4 + 1) * 128], actT[:, kg * 8 + kk, :],
                               kg == 0 and kk == 0, kg == 2 and kk == nkg - 1, [T_actT, T_W], [R4[j4][1]])
                for j4 in range(4):
                    oc = cpn * 4 + j4
                    tt("dve", xt[:, oc, :], R4[j4][0], xt[:, oc, :], ALU.add, [R4[j4][1], T_x], [T_x])
            mark('G%d_plenorm' % G)
            make_hT(xt, T_x, 16)
            mark('G%d_ple' % G)
            for jp in range(2):
                W, T_W = wnext()
                for j4 in range(4):
                    oc = jp * 4 + j4
                    pY, T_pY = bank()
                    for k in range(KC):
                        mm(pY, W[:, k, j4 * 128:(j4 + 1) * 128], hbuf[:, k, :], k == 0, k == KC - 1,
                           [T_h, T_W], [T_pY])
                    pZ, T_pZ = bank()
                    for kk in range(2):
                        mm(pZ, Wpp[:, kk, oc * 128:(oc + 1) * 128], pt_bf[:, kk, :], kk == 0, kk == 1,
                           [T_ptbf, T_Wpp], [T_pZ])
                    ta, T_ta = t1b[oc % 2], T_t1[oc % 2]
                    tb_, T_tb = t2b[oc % 2], T_t2[oc % 2]
                    act(ta[:], pY, AF.Sigmoid, [T_pY], [T_ta])
                    tt("dve", tb_[:], pZ, ta[:], ALU.mult, [T_pZ, T_ta], [T_tb])
                    tt("dve", xt[:, oc, :], xt[:, oc, :], tb_[:], ALU.add, [T_x, T_tb], [T_x])
            mark('G%d_final' % G)
            norm_sq(xt, T_x)
            for k in range(KC):
                stt(xt[:, k, :], xt[:, k, :], gains[:, 24 + k:25 + k], rstd[:], ALU.mult, ALU.mult,
                    [T_x, T_rstd, T_const], [T_x])
            dma("pool", outT.rearrange("(k p) n -> p k n", p=128)[:, :, q0:q0 + 512], xt[:], [T_x], [T_out],
                T_xst[(NT + G) % 2])

    mark('end')
    nc._marks = marks
    fin = st.enter_context(nc.semaphore("fin"))
    blk = st.enter_context(nc.Block())
    emap = {"sp": blk.sync, "act": blk.scalar, "dve": blk.vector, "pool": blk.gpsimd, "pe": blk.tensor}

    def handles(run):
        for e in Sched.ENGS:
            def f(h, e=e):
                run(e, h)
            emap[e](f)

    with nc.allow_low_precision(reason="bf16 matmul operands, fp32 accumulation"):
        sch.emit(nc, st, handles)
    st.close()
    return nc


def _rope_tab(S, d):
    half = d // 2
    inv = (1.0 / (np.float32(10000.0) ** (np.arange(half, dtype=np.float32) * np.float32(2.0 / d)))).astype(np.float32)
    ang = (np.arange(S, dtype=np.float32)[:, None] * inv[None, :]).astype(np.float32)
    return np.cos(ang).astype(np.float32), np.sin(ang).astype(np.float32)


def own_index(S, r):
    idx = []
    for G in range(S // 1024):
        for o in OFFS[r]:
            c = 8 * G + o
            idx.append(np.arange(c * 128, (c + 1) * 128))
    return np.concatenate(idx)


def prep(inputs, S, B):
    bf = ml_dtypes.bfloat16
    f32 = np.float32
    x = np.asarray(inputs["x"], f32)
    p = np.asarray(inputs["p"], f32)[0]
    w_in = np.asarray(inputs["w_in"], f32)[0]
    c64, s64 = _rope_tab(S, 64)
    c32, s32 = _rope_tab(S, 32)
    pp = np.arange(128)
    f64 = (pp % 64) % 32
    sg64 = np.where((pp % 64) < 32, -1.0, 1.0).astype(f32)
    f32i = (pp % 32) % 16
    sg32 = np.where((pp % 32) < 16, -1.0, 1.0).astype(f32)
    tabK = np.empty((128, 4, S), f32)
    tabK[:, 0] = c64[:, f64].T
    tabK[:, 1] = s64[:, f64].T
    tabK[:, 2] = c32[:, f32i].T
    tabK[:, 3] = s32[:, f32i].T
    neg64 = sg64 < 0
    neg32 = sg32 < 0
    tabK[neg64, 1] = -tabK[neg64, 1]
    tabK[neg32, 3] = -tabK[neg32, 3]
    onehot = np.zeros((32, S), bf)
    for n in range(S // 256):
        onehot[n, n * 256:(n + 1) * 256] = 1
    ident = np.eye(128).astype(bf)
    gains = np.concatenate([np.asarray(inputs[k], f32).reshape(8, 128).T for k in
                            ("attn_norm", "ffn_norm", "ple_norm", "final_norm")] +
                           [np.asarray(inputs["mla_kv_norm"], f32).reshape(1, 128).T,
                            np.asarray(inputs["mla_q_norm"], f32).reshape(2, 128).T], axis=1)

    def swap_halves(w, hd):
        k_, n_ = w.shape
        v = w.reshape(k_, n_ // hd, 2, hd // 2)
        return np.ascontiguousarray(v[:, :, ::-1, :]).reshape(k_, n_)

    w_qa, w_ka, w_va = w_in[:, 0:512], w_in[:, 512:1024], w_in[:, 1024:1536]
    w_cq, w_ckv, w_kr = w_in[:, 1536:1792], w_in[:, 1792:1920], w_in[:, 1920:1952]
    w_qb0 = np.asarray(inputs["w_q_b"], f32)[0]
    qb3 = w_qb0.reshape(256, 8, 96)
    qb_sw = np.concatenate([qb3[:, :, 0:64], qb3[:, :, 80:96], qb3[:, :, 64:80]], axis=2).reshape(256, 768)
    w_kvb = np.asarray(inputs["w_kv_b"], f32)[0].reshape(128, 8, 128)
    common = {
        "tabK": tabK, "onehot": onehot, "ident": ident,
        "gains": np.ascontiguousarray(gains),
        "w_kside": np.ascontiguousarray(np.concatenate(
            [w_ka, swap_halves(w_ka, 64), w_va, w_ckv, w_kr, swap_halves(w_kr, 32)], axis=1)),
        "w_qside": np.ascontiguousarray(np.concatenate([w_qa, swap_halves(w_qa, 64), w_cq], axis=1)),
        "w_gates": np.ascontiguousarray(w_in[:, 1952:4000]),
        "w_kvb_k": np.ascontiguousarray(w_kvb[:, :, 0:64].reshape(128, 512)),
        "w_kvb_v": np.ascontiguousarray(w_kvb[:, :, 64:128].reshape(128, 512)),
        "w_qb": np.ascontiguousarray(np.concatenate([w_qb0, qb_sw], axis=1)),
        "w_ba": np.asarray(inputs["w_moba_branch"], f32)[0],
        "w_bb": np.asarray(inputs["w_mla_branch"], f32)[0],
        "w_o": np.asarray(inputs["w_o"], f32)[0],
        "w_gu": np.asarray(inputs["w_gate_up"], f32)[0],
        "w_dn": np.asarray(inputs["w_down"], f32)[0],
        "w_pg": np.asarray(inputs["w_ple_gate"], f32)[0],
        "w_pp": np.asarray(inputs["w_ple_proj"], f32)[0],
    }
    diags = []
    kk = np.arange(128)[:, None]
    qq = np.arange(128)[None, :]
    tri = np.where(kk <= qq, 0.0, NEG).astype(np.float32)
    for r in range(2):
        dg = np.zeros((128, 8, 512), np.float32)
        for j in range(8):
            for s, o in enumerate(OFFS[r]):
                blk = tri if j == o else (np.zeros((128, 128), np.float32) if j < o else
                                          np.full((128, 128), NEG, np.float32))
                dg[:, j, s * 128:(s + 1) * 128] = blk
        diags.append(dg.astype(bf))
    in_maps = []
    for b in range(B):
        xT_all = np.ascontiguousarray(x[b].T)
        for r in range(2):
            oi = own_index(S, r)
            m = dict(common)
            m["xT_all"] = xT_all
            m["xT_own"] = np.ascontiguousarray(x[b][oi].T)
            m["pT_own"] = np.ascontiguousarray(p[b][oi].T)
            m["tabQ"] = np.ascontiguousarray(tabK[:, :, oi])
            m["diag"] = diags[r]
            in_maps.append(m)
    return in_maps


_NC_CACHE = {}


def kernel(**inputs):
    x = np.asarray(inputs["x"])
    B, S, _ = x.shape
    in_maps = prep(inputs, S, B)
    key = (S,)
    if key not in _NC_CACHE:
        _NC_CACHE[key] = build(S)
    nc = _NC_CACHE[key]
    res = run_bass_kernel_spmd(nc, in_maps, core_ids=list(range(2 * B)))
    out = np.empty((B, S, D), np.float32)
    i = 0
    for b in range(B):
        for r in range(2):
            oi = own_index(S, r)
            out[b, oi, :] = np.asarray(res.results[i]["outT"]).T
            i += 1
    return out
```
